# Optimizing a Trainium2 kernel written in Bass

```python
import jax, jax.numpy as jnp
from jax import lax
import numpy as np

D_MODEL = 2048
BATCH = 1
SEQ = 8192
DEPTH = 4

GRID_W = 64
MEM_LEN = 256
EPS = 1e-6

NA_HEADS = 8
NA_HEAD_DIM = 128
NA_WIDTH = NA_HEADS * NA_HEAD_DIM
NA_ROWS_MAX = 8
NA_COLS = 16

MLA_HEADS = 8
MLA_Q_RANK = 512
MLA_KV_RANK = 256
MLA_NOPE_DIM = 128
MLA_ROPE_DIM = 64
MLA_V_DIM = 128
MLA_QK_DIM = MLA_NOPE_DIM + MLA_ROPE_DIM
MLA_WIDTH = MLA_HEADS * MLA_V_DIM
ROPE_THETA = 10000.0
Q_BLOCK = 128

D_MIX = NA_WIDTH + MLA_WIDTH
IN_WIDTH = 3 * NA_WIDTH + MLA_Q_RANK + MLA_KV_RANK + MLA_ROPE_DIM

X_HEADS = 4
X_HEAD_DIM = D_MODEL // X_HEADS

D_FF = ((8 * D_MODEL + 3 * 256 - 1) // (3 * 256)) * 256

kernel_name = "hybrid_natten_mla_encoder"


def rmsnorm(x, g):
    xf = x.astype(jnp.float32)
    y = xf * lax.rsqrt(jnp.mean(xf * xf, axis=-1, keepdims=True) + EPS)
    return (y * g.astype(jnp.float32)).astype(x.dtype)


def rope_tables(seq_len):
    inv = 1.0 / (ROPE_THETA ** (jnp.arange(0, MLA_ROPE_DIM, 2, dtype=jnp.float32) / MLA_ROPE_DIM))
    ang = jnp.arange(seq_len, dtype=jnp.float32)[:, None] * inv[None, :]
    return jnp.cos(ang), jnp.sin(ang)


def apply_rope(x, cos, sin):
    xf = x.astype(jnp.float32)
    x1, x2 = jnp.split(xf, 2, axis=-1)
    return jnp.concatenate([x1 * cos - x2 * sin, x1 * sin + x2 * cos], axis=-1).astype(x.dtype)


def neighbourhood_attention(q, k, v, rpb):
    B, S, H, dh = q.shape
    rows = S // GRID_W
    kr = min(NA_ROWS_MAX, rows)
    qg = q.reshape(B, rows, GRID_W, H, dh)
    kg = k.reshape(B, rows, GRID_W, H, dh)
    vg = v.reshape(B, rows, GRID_W, H, dh)
    row_start = jnp.clip(jnp.arange(rows) - kr // 2, 0, rows - kr)
    col_start = np.clip(np.arange(GRID_W) - NA_COLS // 2, 0, GRID_W - NA_COLS)
    col_idx = col_start[:, None] + np.arange(NA_COLS)[None, :]
    dc = col_idx - np.arange(GRID_W)[:, None] + (NA_COLS - 1)
    bias_c = rpb[:, :, dc]
    scale = dh ** -0.5

    def one_row(r):
        rs = row_start[r]
        kb = lax.dynamic_slice_in_dim(kg, rs, kr, axis=1)[:, :, col_idx]
        vb = lax.dynamic_slice_in_dim(vg, rs, kr, axis=1)[:, :, col_idx]
        qr = lax.dynamic_index_in_dim(qg, r, axis=1, keepdims=False)
        s = jnp.einsum('bqhd,brqkhd->bhqrk', qr, kb).astype(jnp.float32) * scale
        dr = rs + jnp.arange(kr) - r + (NA_ROWS_MAX - 1)
        bias = jnp.transpose(bias_c[:, dr], (0, 2, 1, 3)).astype(jnp.float32)
        s = s + bias[None]
        p = jax.nn.softmax(s.reshape(B, H, GRID_W, kr * NA_COLS), axis=-1)
        p = p.reshape(B, H, GRID_W, kr, NA_COLS).astype(v.dtype)
        return jnp.einsum('bhqrk,brqkhd->bqhd', p, vb)

    out = lax.map(one_row, jnp.arange(rows))
    return jnp.moveaxis(out, 0, 1).reshape(B, S, H * dh)


def latent_attention(c_q, c_kv, k_rope, q_norm, kv_norm, w_uq, w_ukv, cos, sin):
    B, S, _ = c_q.shape
    q = (rmsnorm(c_q, q_norm) @ w_uq).reshape(B, S, MLA_HEADS, MLA_QK_DIM)
    q_nope, q_rope = q[..., :MLA_NOPE_DIM], q[..., MLA_NOPE_DIM:]
    q_rope = apply_rope(q_rope, cos[:, None, :], sin[:, None, :])
    kv = (rmsnorm(c_kv, kv_norm) @ w_ukv).reshape(B, S, MLA_HEADS, MLA_NOPE_DIM + MLA_V_DIM)
    k_nope, v = kv[..., :MLA_NOPE_DIM], kv[..., MLA_NOPE_DIM:]
    k_r = apply_rope(k_rope, cos, sin)
    nb = S // Q_BLOCK
    qn_b = jnp.moveaxis(q_nope.reshape(B, nb, Q_BLOCK, MLA_HEADS, MLA_NOPE_DIM), 1, 0)
    qr_b = jnp.moveaxis(q_rope.reshape(B, nb, Q_BLOCK, MLA_HEADS, MLA_ROPE_DIM), 1, 0)
    scale = MLA_QK_DIM ** -0.5

    def block(args):
        qn, qr = args
        s = (jnp.einsum('bqhd,bkhd->bhqk', qn, k_nope)
             + jnp.einsum('bqhr,bkr->bhqk', qr, k_r)).astype(jnp.float32) * scale
        p = jax.nn.softmax(s, axis=-1).astype(v.dtype)
        return jnp.einsum('bhqk,bkhd->bqhd', p, v)

    out = lax.map(block, (qn_b, qr_b))
    return jnp.moveaxis(out, 0, 1).reshape(B, S, MLA_WIDTH)


def memory_attention(h, mem_n, w_q, w_k, w_v, w_o):
    B, S, _ = h.shape
    M = mem_n.shape[1]
    q = (h @ w_q).reshape(B, S, X_HEADS, X_HEAD_DIM)
    k = (mem_n @ w_k).reshape(B, M, X_HEADS, X_HEAD_DIM)
    v = (mem_n @ w_v).reshape(B, M, X_HEADS, X_HEAD_DIM)
    s = jnp.einsum('bqhd,bkhd->bhqk', q, k).astype(jnp.float32) * (X_HEAD_DIM ** -0.5)
    p = jax.nn.softmax(s, axis=-1).astype(v.dtype)
    o = jnp.einsum('bhqk,bkhd->bqhd', p, v).reshape(B, S, D_MODEL)
    return o @ w_o


def swiglu(h, w_gate, w_up, w_down):
    return (jax.nn.silu(h @ w_gate) * (h @ w_up)) @ w_down


def setup_inputs(seed: int = 0) -> dict:
    key = jax.random.key(seed)
    ks = jax.random.split(key, 24)

    def dense(k, shape, fan_in):
        return jax.random.normal(k, shape, jnp.float32) * (fan_in ** -0.5)

    def gain(k, shape):
        return 1.0 + 0.02 * jax.random.normal(k, shape, jnp.float32)

    L = DEPTH
    return {
        "x": jax.random.normal(ks[0], (BATCH, SEQ, D_MODEL), jnp.float32),
        "mem": jax.random.normal(ks[1], (BATCH, MEM_LEN, D_MODEL), jnp.float32),
        "ln_mix": gain(ks[2], (L, D_MODEL)),
        "w_in": dense(ks[3], (L, D_MODEL, IN_WIDTH), D_MODEL),
        "q_norm": gain(ks[4], (L, MLA_Q_RANK)),
        "kv_norm": gain(ks[5], (L, MLA_KV_RANK)),
        "w_uq": dense(ks[6], (L, MLA_Q_RANK, MLA_HEADS * MLA_QK_DIM), MLA_Q_RANK),
        "w_ukv": dense(ks[7], (L, MLA_KV_RANK, MLA_HEADS * (MLA_NOPE_DIM + MLA_V_DIM)), MLA_KV_RANK),
        "na_rpb": 0.1 * jax.random.normal(ks[8], (L, NA_HEADS, 2 * NA_ROWS_MAX - 1, 2 * NA_COLS - 1), jnp.float32),
        "na_out_norm": gain(ks[9], (L, NA_WIDTH)),
        "mla_out_norm": gain(ks[10], (L, MLA_WIDTH)),
        "w_out": dense(ks[11], (L, D_MIX, D_MODEL), D_MIX),
        "ln_mem": gain(ks[12], (L, D_MODEL)),
        "mem_norm": gain(ks[13], (L, D_MODEL)),
        "w_xq": dense(ks[14], (L, D_MODEL, D_MODEL), D_MODEL),
        "w_xk": dense(ks[15], (L, D_MODEL, D_MODEL), D_MODEL),
        "w_xv": dense(ks[16], (L, D_MODEL, D_MODEL), D_MODEL),
        "w_xo": dense(ks[17], (L, D_MODEL, D_MODEL), D_MODEL),
        "ln_ffn": gain(ks[18], (L, D_MODEL)),
        "w_gate": dense(ks[19], (L, D_MODEL, D_FF), D_MODEL),
        "w_up": dense(ks[20], (L, D_MODEL, D_FF), D_MODEL),
        "w_down": dense(ks[21], (L, D_FF, D_MODEL), D_FF),
        "final_norm": gain(ks[22], (D_MODEL,)),
    }


def reference(x, mem, ln_mix, w_in, q_norm, kv_norm, w_uq, w_ukv, na_rpb, na_out_norm,
              mla_out_norm, w_out, ln_mem, mem_norm, w_xq, w_xk, w_xv, w_xo, ln_ffn,
              w_gate, w_up, w_down, final_norm):
    B, S, _ = x.shape
    cos, sin = rope_tables(S)
    o1 = NA_WIDTH
    o2 = 2 * NA_WIDTH
    o3 = 3 * NA_WIDTH
    o4 = o3 + MLA_Q_RANK
    o5 = o4 + MLA_KV_RANK
    for l in range(DEPTH):
        h = rmsnorm(x, ln_mix[l])
        proj = h @ w_in[l]
        q_na = proj[..., :o1].reshape(B, S, NA_HEADS, NA_HEAD_DIM)
        k_na = proj[..., o1:o2].reshape(B, S, NA_HEADS, NA_HEAD_DIM)
        v_na = proj[..., o2:o3].reshape(B, S, NA_HEADS, NA_HEAD_DIM)
        c_q = proj[..., o3:o4]
        c_kv = proj[..., o4:o5]
        k_rope = proj[..., o5:]
        y_na = neighbourhood_attention(q_na, k_na, v_na, na_rpb[l])
        y_mla = latent_attention(c_q, c_kv, k_rope, q_norm[l], kv_norm[l], w_uq[l], w_ukv[l], cos, sin)
        y = jnp.concatenate([rmsnorm(y_na, na_out_norm[l]), rmsnorm(y_mla, mla_out_norm[l])], axis=-1)
        x = x + y @ w_out[l]
        h = rmsnorm(x, ln_mem[l])
        x = x + memory_attention(h, rmsnorm(mem, mem_norm[l]), w_xq[l], w_xk[l], w_xv[l], w_xo[l])
        h = rmsnorm(x, ln_ffn[l])
        x = x + swiglu(h, w_gate[l], w_up[l], w_down[l])
    return rmsnorm(x, final_norm)
```

```python
import numpy as np
import ml_dtypes
from contextlib import ExitStack
import concourse.bass as bass
import concourse.mybir as mybir
from concourse.bass_utils import run_bass_kernel_spmd

F32 = mybir.dt.float32
BF16 = mybir.dt.bfloat16
I32 = mybir.dt.int32
AF = mybir.ActivationFunctionType
ALU = mybir.AluOpType

NC_ = 8
D = 2048
S = 8192
T = 1024
L = 4
KC = 16
DFF = 5632
INW = 3904
EPS = 1e-6
NA_SCALE = 128 ** -0.5
MLA_SCALE = 192 ** -0.5
X_SCALE = 512 ** -0.5
NEG = -1.0e5

SZA = 256 * INW + 512 * 192 + 256 * 256
SZB = 5 * 256 * 2048
SZC = 2 * 256 * DFF + DFF * 256
OFF_WIN, OFF_WUQ, OFF_WUKV = 0, 256 * INW, 256 * INW + 512 * 192
OFF_WOUT, OFF_WXQ, OFF_WXK, OFF_WXV, OFF_WXO = [i * 256 * 2048 for i in range(5)]
OFF_WG, OFF_WU, OFF_WD = 0, 256 * DFF, 2 * 256 * DFF
X_CKV, X_KR, X_KNA, X_VNA = 0, 256 * 1024, 320 * 1024, 320 * 1024 + 8 * 128 * 512
SZX = X_VNA + 512 * 1024
NG_L = 86
G_MIX, G_MEM, G_MEMN, G_FFN, G_QN, G_KVN, G_NAO, G_MLAO = 0, 16, 32, 48, 64, 68, 70, 78
NG = L * NG_L + 16
GR = 256


class Op:
    __slots__ = ("eng", "fn", "deps", "grp", "amt", "idx", "needs_inc", "cwaits", "dwaits",
                 "dma_waits", "count", "epoch", "vc")


class Prog:
    CE = ("pe", "act", "dve", "pool", "sp")
    EI = {"pe": 0, "act": 1, "dve": 2, "pool": 3, "sp": 4}

    def __init__(self):
        self.q = {e: [] for e in self.CE}
        self.res = {}
        self.all = []
        self.epoch = 0
        self.grp_total = {}

    def add(self, eng, fn, reads=(), writes=(), grp=None, amt=16):
        psr = [k for k in reads if k[0] == "ps"]
        if psr:
            reads = [k for k in reads if k[0] != "ps"]
            writes = list(writes) + psr
        op = Op()
        op.eng, op.fn, op.grp, op.amt, op.epoch = eng, fn, grp, amt, self.epoch
        op.needs_inc = False
        deps = []
        for k in reads:
            st = self.res.get(k)
            if st is not None and st[0] is not None:
                deps.append(st[0])
        for k in writes:
            st = self.res.get(k)
            if st is not None:
                if st[0] is not None:
                    deps.append(st[0])
                deps.extend(st[1].values())
                deps.extend(st[2])
        dmaw = {}
        cdeps = []
        for d in deps:
            if d is op:
                continue
            if d.grp is not None:
                dmaw[d.grp] = self.grp_total[d.grp]
            else:
                cdeps.append(d)
        op.deps = cdeps
        op.dma_waits = dmaw
        for k in reads:
            st = self.res.get(k)
            if st is None:
                st = self.res[k] = [None, {}, []]
            if grp is None:
                st[1][eng] = op
            else:
                st[2].append(op)
        for k in writes:
            self.res[k] = [op, {}, []]
        self.all.append(op)
        op.idx = len(self.q[eng])
        self.q[eng].append(op)
        if grp is not None:
            self.grp_total[grp] = self.grp_total.get(grp, 0) + amt
        return op

    def finalize(self):
        known = {e: [-1] * 5 for e in self.CE}
        dknown = {e: {} for e in self.CE}
        for op in self.all:
            E = op.eng
            kn = known[E]
            cw = []
            for d in op.deps:
                if d.eng == "pe" and E == "pe":
                    continue
                di = self.EI[d.eng]
                if kn[di] >= d.idx:
                    continue
                cw.append(d)
                d.needs_inc = True
                vc = d.vc
                for i in range(5):
                    if vc[i] > kn[i]:
                        kn[i] = vc[i]
            op.cwaits = cw
            dw = {}
            dk = dknown[E]
            for g, v in op.dma_waits.items():
                if dk.get(g, 0) >= v:
                    continue
                dk[g] = v
                dw[g] = v
            op.dwaits = dw
            if op.grp is None:
                vc = list(kn)
                ei = self.EI[E]
                if op.idx > vc[ei]:
                    vc[ei] = op.idx
                op.vc = tuple(vc)
            else:
                op.vc = None
        cnt = {}
        for e in self.CE:
            for op in self.q[e]:
                if op.grp is None and op.needs_inc:
                    key = (e, op.epoch)
                    cnt[key] = cnt.get(key, 0) + 1
                    op.count = cnt[key]
        return cnt

    def emit(self, nc, es):
        cnt = self.finalize()
        csem = {}
        for key in cnt:
            csem[key] = es.enter_context(nc.semaphore("c_%s_%d" % key))
        dsem = {}
        for g in self.grp_total:
            dsem[g] = es.enter_context(nc.semaphore("d_%s" % (str(g).replace(" ", ""))))
        self.nsem = len(csem) + len(dsem)
        block = es.enter_context(nc.Block())

        def run(eng_name):
            def body(eng):
                for op in self.q[eng_name]:
                    for d in op.cwaits:
                        eng.wait_ge(csem[(d.eng, d.epoch)], d.count)
                    for g, v in op.dwaits.items():
                        eng.wait_ge(dsem[g], v)
                    if op.fn is None:
                        continue
                    ins = op.fn(eng)
                    if op.grp is not None:
                        ins.then_inc(dsem[op.grp], op.amt)
                    elif op.needs_inc:
                        ins.then_inc(csem[(eng_name, op.epoch)], 1)
            return body

        block.tensor(run("pe"))
        block.scalar(run("act"))
        block.vector(run("dve"))
        block.gpsimd(run("pool"))
        block.sync(run("sp"))


class V:
    __slots__ = ("ap", "keys")

    def __init__(self, ap, keys):
        self.ap, self.keys = ap, keys


class Tile:
    def __init__(self, arena_ap, arena_name, off, shape, dt, parts=128):
        self.esz = 4 if dt in (F32, I32) else 2
        n = int(np.prod(shape))
        nb = n * self.esz
        assert off % 4 == 0 and off + nb <= arena_ap.shape[1] * 2, (arena_name, off, nb)
        a = arena_ap[0:parts, off // 2: off // 2 + nb // 2]
        if self.esz == 4:
            a = a.bitcast(dt)
        if len(shape) == 2:
            a = a.rearrange("p (a b) -> p a b", a=shape[0])
        elif len(shape) == 3:
            a = a.rearrange("p (a b c) -> p a b c", a=shape[0], b=shape[1])
        self.ap, self.off, self.shape, self.arena = a, off, list(shape), arena_name

    def kr(self, lo, n):
        b0 = self.off + lo * self.esz
        b1 = b0 + n * self.esz
        return [(self.arena, g) for g in range(b0 // GR, (b1 - 1) // GR + 1)]

    def v(self, *idx):
        sh = self.shape
        ints = [i for i in idx if not isinstance(i, tuple)]
        rng = [i for i in idx if isinstance(i, tuple)]
        ap = self.ap
        sl = [slice(None)] + list(ints)
        lo_e = 0
        stride = int(np.prod(sh))
        for d, i in enumerate(ints):
            stride //= sh[d]
            lo_e += i * stride
        n = stride
        if rng:
            lo, hi = rng[0]
            sl.append(slice(lo, hi))
            inner = stride // sh[len(ints)]
            lo_e += lo * inner
            n = (hi - lo) * inner
        ap = ap[tuple(sl)]
        return V(ap, self.kr(lo_e, n))


class Ctx:
    def __init__(self, nslot=4):
        self.nc = nc = bass.Bass("TRN2", target_bir_lowering=False)
        self.P = Prog()
        self.es = ExitStack()
        self.rot = {}
        self.outs = []
        self.NSLOT = nslot
        self.wn = 0
        self.RING = self.sb("RING", [128, nslot * 4096], BF16)
        self.M = self.sb("M", [128, 8192], BF16)
        self.ps = [self.es.enter_context(nc.psum_tensor("ps%d" % i, [128, 512], F32)) for i in range(8)]
        M_ap = self.M[:, :]
        self.stg = Tile(M_ap, "M", 0, [2, 512], BF16)
        self.sq = Tile(M_ap, "M", 2048, [2, 512], BF16)
        self.Pt = Tile(M_ap, "M", 4096, [3, 512], BF16)
        self.tmp = Tile(M_ap, "M", 7168, [2, 512], F32)
        self.rr = Tile(M_ap, "M", 11264, [2, 512], F32)
        self.qrb = Tile(M_ap, "M", 15360, [512], BF16)
        self.cbf_t = self.sb("cbf_sb", [128, 320], BF16)
        self.ones = V(self.cbf_t[:, 0:128], [("cbf",)])
        self.ident = V(self.cbf_t[:, 128:256], [("cbf",)])
        self.rmat = V(self.cbf_t[0:64, 256:320], [("cbf",)])

    def din(self, name, shape, dt):
        return self.nc.dram_tensor(name, shape, dt, kind="ExternalInput")

    def dout(self, name, shape, dt):
        t = self.nc.dram_tensor(name, shape, dt, kind="ExternalOutput")
        self.outs.append(name)
        return t

    def sb(self, name, shape, dt):
        return self.es.enter_context(self.nc.sbuf_tensor(name, shape, dt))

    def nxt(self, name, n):
        self.rot[name] = (self.rot.get(name, -1) + 1) % n
        return self.rot[name]

    def PS(self, b):
        return V(self.ps[b][:, :], [("ps", b)])

    def PSs(self, b, p0, p1, c0, c1):
        return V(self.ps[b][p0:p1, c0:c1], [("ps", b)])

    def mm(self, out, lhsT, rhs, start, stop):
        self.P.add("pe", lambda e, o=out.ap, a=lhsT.ap, b=rhs.ap, s=start, t=stop: e.matmul(o, a, b, start=s, stop=t),
                   reads=lhsT.keys + rhs.keys, writes=out.keys)

    def act(self, out, in_, func, scale=1.0):
        self.P.add("act", lambda e, o=out.ap, i=in_.ap, f=func, s=scale: e.activation(o, i, f, scale=s),
                   reads=in_.keys, writes=out.keys)

    def tt(self, out, a, b, op, eng="dve"):
        self.P.add(eng, lambda e, o=out.ap, x=a.ap, y=b.ap, p=op: e.tensor_tensor(o, x, y, p),
                   reads=a.keys + b.keys, writes=out.keys)

    def ts(self, out, a, s1, s2, op0, op1, extra=()):
        self.P.add("dve", lambda e, o=out.ap, x=a.ap, u=s1, w=s2, p=op0, q=op1: e.tensor_scalar(o, x, u, w, p, q),
                   reads=a.keys + list(extra), writes=out.keys)

    def stt(self, out, a, scalar, b, op0, op1, extra=()):
        self.P.add("dve", lambda e, o=out.ap, x=a.ap, s=scalar, y=b.ap, p=op0, q=op1: e.scalar_tensor_tensor(o, x, s, y, p, q),
                   reads=a.keys + b.keys + list(extra), writes=out.keys)

    def copy(self, out, a, eng="dve"):
        self.P.add(eng, lambda e, o=out.ap, x=a.ap: e.tensor_copy(o, x), reads=a.keys, writes=out.keys)

    def recip(self, out, a):
        self.P.add("dve", lambda e, o=out.ap, x=a.ap: e.reciprocal(o, x), reads=a.keys, writes=out.keys)

    def dma(self, eng, out_ap, in_ap, reads, writes, grp):
        self.P.add(eng, lambda e, o=out_ap, i=in_ap: e.dma_start(out=o, in_=i), reads=reads, writes=writes, grp=grp)

    def load_consts(self, cbf_d):
        self.dma("sp", self.cbf_t[:, :], cbf_d.ap()[:, :], [], [("cbf",)], grp="cin")

    def rstd_from(self, r, acc, Dn):
        self.ts(r, acc, 1.0 / Dn, EPS, ALU.mult, ALU.add)
        self.act(r, r, AF.Sqrt)
        self.recip(r, r)

    def rms_stats(self, src_fn, nk, Dn, hf, width=512):
        b = self.nxt("ps", 2)
        acc = self.PSs(b, 0, 128, 0, width)
        for k in range(nk):
            s0 = self.sq.v(self.nxt("sq", 2))
            s = V(s0.ap[:, 0:width], s0.keys)
            self.act(s, src_fn(k, hf), AF.Square)
            self.mm(acc, self.ones, s, k == 0, k == nk - 1)
        r0 = self.rr.v(self.nxt("rr", 2))
        r = V(r0.ap[:, 0:width], r0.keys)
        self.rstd_from(r, acc, Dn)
        return r

    def wtile(self, W_ap, c0, n, kc=16):
        s = self.wn % self.NSLOT
        self.wn += 1
        base = self.RING[:, s * 4096: s * 4096 + kc * n]
        dst = base.rearrange("p (k c) -> p k c", k=kc)
        src = W_ap[:, c0:c0 + n].rearrange("(k p) c -> p k c", p=128)
        keys = [("RING", s)]
        self.dma("pool", dst, src, [], keys, grp="ring%d" % s)
        return V(dst, keys)

    def finish(self):
        P = self.P
        P.add("sp", None, reads=[k for k in P.res if k[0] == "out"], writes=[])
        P.emit(self.nc, self.es)
        self.es.close()
        return self.nc


def _xv(X_t):
    return lambda k, hf: V(X_t[:, k, hf * 512:(hf + 1) * 512], [("X", k, hf)])


def _load_x(c, X_t, xT_d):
    for hf in range(2):
        for k in range(KC):
            q = ("sp", "act")[k % 2]
            c.dma(q, X_t[:, k, hf * 512:(hf + 1) * 512], xT_d.ap()[k * 128:(k + 1) * 128, hf * 512:(hf + 1) * 512], [],
                  [("X", k, hf)], grp="xin%d" % (k % 2))


def _rmsnorm_x(c, xv, gains_t, gbase, dst):
    for hf in range(2):
        r = c.rms_stats(xv, KC, D, hf)
        for k in range(KC):
            c.stt(dst.v(k, (hf * 512, hf * 512 + 512)), xv(k, hf), gains_t[:, gbase + k:gbase + k + 1], r,
                  ALU.mult, ALU.mult, extra=[("gains",)])


def build_k1():
    c = Ctx(nslot=4)
    xT_d = c.din("xT", [D, T], F32)
    w_in = c.din("w_in", [D, INW], F32)
    gains_d = c.din("gains", [128, NG_L], F32)
    cs_d = c.din("cs", [64, T], F32)
    sn_d = c.din("sn", [64, T], F32)
    cbf_d = c.din("cbf", [128, 320], BF16)
    wukv_d = c.din("w_ukv", [256, 2048], F32)
    kmla_o = c.dout("kmla", [1024, T], BF16)
    vmla_o = c.dout("vmla", [T, 1024], BF16)
    qna_o = c.dout("qna", [1024, T], BF16)
    kna_o = c.dout("kna", [1024, T], BF16)
    vna_o = c.dout("vna", [T, 1024], BF16)
    cqn_o = c.dout("cqn", [512, T], BF16)
    ckvn_o = c.dout("ckvn", [256, T], BF16)
    kr_o = c.dout("kr", [64, T], BF16)
    X_t = c.sb("X", [128, KC, T], F32)
    A_t = c.sb("A", [128, 16384], BF16)
    C_t = c.sb("C", [128, 8192], BF16)
    gains_t = c.sb("gains_sb", [128, NG_L], F32)
    wukv_t = c.sb("wukv_sb", [128, 2, 2048], BF16)
    ostg_t = c.sb("ostg_sb", [128, 4, 1024], BF16)
    c.dma("pool", wukv_t[:, :, :], wukv_d.ap().rearrange("(k p) c -> p k c", p=128), [], [("wukv",)], grp="cinw")

    def ostg(hf):
        i = c.rot.get("os", 0)
        if hf == 0:
            i = c.nxt("os", 4)
        return i, V(ostg_t[:, i, hf * 512:(hf + 1) * 512], [("ostg", i, hf)])

    def oq():
        return ("sp", "act")[c.nxt("oq", 2)]

    hT = Tile(A_t[:, :], "A", 0, [KC, T], BF16)
    cq16 = Tile(C_t[:, :], "C", 0, [4, T], BF16)
    ckv16 = Tile(C_t[:, :], "C", 8192, [2, T], BF16)
    kr32 = Tile(C_t[:, :], "C", 12288, [T], F32, parts=64)
    cst = Tile(c.M[:, :], "M", 0, [512], F32)
    snt = Tile(c.M[:, :], "M", 2048, [512], F32)
    xv = _xv(X_t)
    _load_x(c, X_t, xT_d)
    c.dma("sp", gains_t[:, :], gains_d.ap()[:, :], [], [("gains",)], grp="cin")
    c.load_consts(cbf_d)
    _rmsnorm_x(c, xv, gains_t, G_MIX, hT)
    W = w_in.ap()

    def fm_group(wt, oc, hf, M=128):
        b = c.nxt("ps", 2)
        o = c.PSs(b, 0, M, 0, 512)
        for k in range(KC):
            c.mm(o, V(wt.ap[:, k, oc * 128: oc * 128 + M], wt.keys), hT.v(k, (hf * 512, hf * 512 + 512)), k == 0, k == KC - 1)
        return o

    for (c0, n) in ((3072, 256), (3328, 256), (3584, 256), (3840, 64)):
        wt = c.wtile(W, c0, n)
        for oc in range((n + 127) // 128):
            M = min(128, n - oc * 128)
            for hf in range(2):
                o = fm_group(wt, oc, hf, M)
                gi = (c0 - 3072) // 128 + oc
                if gi < 6:
                    s = c.sq.v(c.nxt("sq", 2))
                    c.act(s, o, AF.Square)
                    iscq = gi < 4
                    nk, kk = (4, gi) if iscq else (2, gi - 4)
                    bb = (2 if iscq else 4) + hf
                    c.mm(c.PS(bb), c.ones, s, kk == 0, kk == nk - 1)
                    dst = cq16.v(gi, (hf * 512, hf * 512 + 512)) if iscq else ckv16.v(gi - 4, (hf * 512, hf * 512 + 512))
                    c.copy(dst, o)
                else:
                    c.copy(V(kr32.ap[:, hf * 512:(hf + 1) * 512], kr32.kr(hf * 512, 512)), o)
    for hf in range(2):
        for t16, nk, Dn, gofs, bb, od in ((cq16, 4, 512, G_QN, 2 + hf, cqn_o), (ckv16, 2, 256, G_KVN, 4 + hf, ckvn_o)):
            r = c.rr.v(c.nxt("rr", 2))
            c.rstd_from(r, c.PS(bb), Dn)
            for k in range(nk):
                vv = t16.v(k, (hf * 512, hf * 512 + 512))
                c.stt(vv, vv, gains_t[:, gofs + k:gofs + k + 1], r, ALU.mult, ALU.mult, extra=[("gains",)])
                c.dma("sp", od.ap()[k * 128:(k + 1) * 128, hf * 512:(hf + 1) * 512], vv.ap, vv.keys, [("out", od.name, k, hf)], grp="o1")
        c.dma("sp", cst.ap[0:64, :], cs_d.ap()[:, hf * 512:(hf + 1) * 512], [], cst.kr(0, 512), grp="cst")
        c.dma("sp", snt.ap[0:64, :], sn_d.ap()[:, hf * 512:(hf + 1) * 512], [], snt.kr(0, 512), grp="snt")
        krv = V(kr32.ap[:, hf * 512:(hf + 1) * 512], kr32.kr(hf * 512, 512))
        qb = V(c.qrb.ap[0:64, :], c.qrb.kr(0, 512))
        c.copy(qb, krv)
        o = c.PSs(c.nxt("ps", 2), 0, 64, 0, 512)
        c.mm(o, c.rmat, qb, True, True)
        t1 = V(c.tmp.ap[0:64, 0, :], c.tmp.kr(0, 512))
        t2 = V(c.tmp.ap[0:64, 1, :], c.tmp.kr(512, 512))
        c.tt(t1, krv, V(cst.ap[0:64, :], cst.kr(0, 512)), ALU.mult)
        c.tt(t2, o, V(snt.ap[0:64, :], snt.kr(0, 512)), ALU.mult)
        c.tt(qb, t1, t2, ALU.add)
        c.dma("sp", kr_o.ap()[:, hf * 512:(hf + 1) * 512], qb.ap, qb.keys, [("out", "kr", hf)], grp="o1")
    for base, od, scale in ((0, qna_o, NA_SCALE), (1024, kna_o, 1.0)):
        for cb in range(4):
            wt = c.wtile(W, base + cb * 256, 256)
            for oc in range(2):
                h_ = cb * 2 + oc
                for hf in range(2):
                    o = fm_group(wt, oc, hf)
                    i, s_ = ostg(hf)
                    c.act(s_, o, AF.Copy, scale=scale)
                c.dma(oq(), od.ap()[h_ * 128:(h_ + 1) * 128, :], ostg_t[:, i, :], [("ostg", i, 0), ("ostg", i, 1)],
                      [("out", od.name, h_)], grp="o2")
    for cb in range(4):
        wt = c.wtile(W, 2048 + cb * 256, 256)
        for tc in range(8):
            o = c.PSs(c.nxt("ps", 2), 0, 128, 0, 256)
            for k in range(KC):
                c.mm(o, V(hT.ap[:, k, tc * 128:(tc + 1) * 128], hT.kr(k * T + tc * 128, 128)), V(wt.ap[:, k, :], wt.keys), k == 0, k == KC - 1)
            s = c.stg.v(c.nxt("st", 2))
            sv = V(s.ap[:, 0:256], s.keys)
            c.act(sv, o, AF.Copy)
            c.dma("sp", vna_o.ap()[tc * 128:(tc + 1) * 128, cb * 256:(cb + 1) * 256], sv.ap, sv.keys, [("out", "vna", tc, cb)], grp="o3")
    WK = [("wukv",)]
    for h_ in range(8):
        for hf in range(2):
            o = c.PS(c.nxt("ps6", 6))
            for k in range(2):
                c.mm(o, V(wukv_t[:, k, h_ * 256:h_ * 256 + 128], WK), ckv16.v(k, (hf * 512, hf * 512 + 512)), k == 0, k == 1)
            i, s_ = ostg(hf)
            if hf == 0:
                c.copy(s_, o)
            else:
                c.act(s_, o, AF.Copy)
        c.dma(oq(), kmla_o.ap()[h_ * 128:(h_ + 1) * 128, :], ostg_t[:, i, :], [("ostg", i, 0), ("ostg", i, 1)], [("out", "kmla", h_)], grp="o4")
    wv = wukv_t[:, :, :].rearrange("p k (h c) -> p k h c", c=256)
    for tc in range(8):
        for g in range(2):
            o = c.PS(c.nxt("ps6", 6))
            for k in range(2):
                c.mm(o, V(ckv16.ap[:, k, tc * 128:(tc + 1) * 128], ckv16.kr(k * T + tc * 128, 128)),
                     V(wv[:, k, 4 * g:4 * g + 4, 128:256], WK), k == 0, k == 1)
            i, s_ = ostg(g)
            if g == 0:
                c.copy(s_, o)
            else:
                c.act(s_, o, AF.Copy)
        c.dma(oq(), vmla_o.ap()[tc * 128:(tc + 1) * 128, :], ostg_t[:, i, :], [("ostg", i, 0), ("ostg", i, 1)], [("out", "vmla", tc)], grp="o4")
    return c.finish(), c.outs


def build_k2():
    c = Ctx(nslot=1)
    qna_d = c.din("qna", [1024, T], BF16)
    knah_d = c.din("knah", [1024, 1536], BF16)
    vnah_d = c.din("vnah", [1536, 1024], BF16)
    cqn_d = c.din("cqn", [512, T], BF16)
    kall_d = c.din("k_all", [8, 128, S], BF16)
    vall_d = c.din("v_all", [8, 128, S], BF16)
    kr_d = c.din("kr_all", [64, S], BF16)
    wuq_d = c.din("w_uq", [512, 1536], F32)
    tt_d = c.din("tt", [8, 128, 896], F32)
    rms_d = c.din("rms", [2, 7168], BF16)
    ind2_d = c.din("ind2", [2, 128], BF16)
    cs_d = c.din("cs", [64, T], F32)
    sn_d = c.din("sn", [64, T], F32)
    cbf_d = c.din("cbf", [128, 320], BF16)
    y_o = c.dout("yT", [D, T], BF16)
    U_t = c.sb("U", [128, 39936], BF16)
    cqn_t = c.sb("cqn_sb", [128, 4, T], BF16)
    wuq_t = c.sb("wuq_sb", [128, 4, 1536], BF16)
    kra_t = c.sb("kra_sb", [64, S], BF16)
    qn_t = c.sb("qn_sb", [128, 2, T], BF16)
    qr_t = c.sb("qr_sb", [64, 2, T], BF16)
    cs_t = c.sb("cs_sb", [64, T], F32)
    sn_t = c.sb("sn_sb", [64, T], F32)
    rms_t = c.sb("rms_sb", [2, 7168], BF16)
    ind2_t = c.sb("ind2_sb", [2, 128], BF16)
    U = U_t[:, :]
    qna = Tile(U, "U", 0, [8, T], BF16)
    knah = Tile(U, "U", 16384, [8, 1536], BF16)
    vnah = Tile(U, "U", 40960, [12, 1024], BF16)
    ttb = Tile(U, "U", 65536, [8, 896], BF16)
    KhT = [Tile(U, "U", st_ * 32768, [S], BF16) for st_ in range(2)]
    Vh = [Tile(U, "U", st_ * 32768 + 16384, [64, 128], BF16) for st_ in range(2)]
    CK = [("c2",)]
    c.load_consts(cbf_d)
    c.dma("sp", rms_t[:, :], rms_d.ap()[:, :], [], CK, grp="cin")
    c.dma("sp", ind2_t[:, :], ind2_d.ap()[:, :], [], CK, grp="cin")
    c.dma("sp", cs_t[:, :], cs_d.ap()[:, :], [], CK, grp="cin")
    c.dma("sp", sn_t[:, :], sn_d.ap()[:, :], [], CK, grp="cin")
    for k in range(4):
        c.dma("sp", cqn_t[:, k, :], cqn_d.ap()[k * 128:(k + 1) * 128, :], [], CK, grp="cin")
    c.dma("pool", wuq_t[:, :, :], wuq_d.ap().rearrange("(k p) c -> p k c", p=128), [], CK, grp="cinw")
    for ch in range(12):
        v = vnah.v(ch)
        c.dma(("sp", "act")[ch % 2], v.ap, vnah_d.ap()[ch * 128:(ch + 1) * 128, :], [], v.keys, grp="nain%d" % (ch % 2))
    for h in range(8):
        v = ttb.v(h)
        c.dma("pool", v.ap, tt_d.ap()[h], [], v.keys, grp="nain2")
        v = qna.v(h)
        c.dma("sp", v.ap, qna_d.ap()[h * 128:(h + 1) * 128, :], [], v.keys, grp="nain0")
        v = knah.v(h)
        c.dma("act", v.ap, knah_d.ap()[h * 128:(h + 1) * 128, :], [], v.keys, grp="nain1")
    ind2 = V(ind2_t[:, :], CK)
    ones, ident = c.ones, c.ident

    def out_y(row0, hf, O_b, den_b):
        r = c.rr.v(c.nxt("rr", 2))
        c.recip(r, c.PS(den_b))
        s = c.stg.v(c.nxt("st", 2))
        c.tt(s, c.PS(O_b), r, ALU.mult)
        c.dma("sp", y_o.ap()[row0:row0 + 128, hf * 512:(hf + 1) * 512], s.ap, s.keys, [("out", row0, hf)], grp="yo")

    for h in range(8):
        for rnd in range(2):
            O_b, den_b = 4 + (c.nxt("nao", 2)), 6 + (c.nxt("nad", 2))
            for b in range(4 * rnd, 4 * rnd + 4):
                jl = list(range(1, 6))
                if b == 0:
                    jl = jl + [6]
                if b == 7:
                    jl = [0] + jl
                SA, SB = c.nxt("sa", 2), 2 + c.nxt("sb", 2)
                slot = []
                for si, jj in enumerate(jl):
                    ch = b + jj - 1
                    bank, col = (SA, si * 128) if si < 4 else (SB, (si - 4) * 128)
                    slot.append((bank, col))
                    o = c.PSs(bank, 0, 128, col, col + 128)
                    kk = V(knah.ap[:, h, ch * 128:(ch + 1) * 128], knah.kr(h * 1536 + ch * 128, 128))
                    qq = V(qna.ap[:, h, b * 128:(b + 1) * 128], qna.kr(h * T + b * 128, 128))
                    c.mm(o, kk, qq, True, False)
                    c.mm(o, ident, V(ttb.ap[:, h, jj * 128:(jj + 1) * 128], ttb.kr(h * 896 + jj * 128, 128)), False, False)
                    c.mm(o, ind2, V(rms_t[:, (b * 7 + jj) * 128:(b * 7 + jj + 1) * 128], CK), False, True)
                nB = len(jl) - 4
                pa = c.Pt.v(c.nxt("pt", 3))
                c.act(pa, c.PS(SA), AF.Exp)
                pb0 = c.Pt.v(c.nxt("pt", 3))
                pb = V(pb0.ap[:, 0:nB * 128], pb0.keys)
                c.act(pb, c.PSs(SB, 0, 128, 0, nB * 128), AF.Exp)
                col = (b % 4) * 128
                pvs = [V(pa.ap[:, si * 128:(si + 1) * 128], pa.keys) if si < 4 else V(pb0.ap[:, (si - 4) * 128:(si - 3) * 128], pb0.keys)
                       for si in range(len(jl))]
                for si, jj in enumerate(jl):
                    ch = b + jj - 1
                    vv = V(vnah.ap[:, ch, h * 128:(h + 1) * 128], vnah.kr(ch * 1024 + h * 128, 128))
                    c.mm(c.PSs(O_b, 0, 128, col, col + 128), vv, pvs[si], si == 0, si == len(jl) - 1)
                for si, jj in enumerate(jl):
                    c.mm(c.PSs(den_b, 0, 128, col, col + 128), ones, pvs[si], si == 0, si == len(jl) - 1)
            out_y(h * 128, rnd, O_b, den_b)

    for q4 in range(4):
        c.dma("sp", kra_t[:, q4 * 2048:(q4 + 1) * 2048], kr_d.ap()[:, q4 * 2048:(q4 + 1) * 2048], [], [("kra", q4)], grp="mlain")

    def load_kv(h):
        st = h % 2
        for q4 in range(4):
            v = V(KhT[st].ap[:, q4 * 2048:(q4 + 1) * 2048], KhT[st].kr(q4 * 2048, 2048))
            c.dma("sp", v.ap, kall_d.ap()[h, :, q4 * 2048:(q4 + 1) * 2048], [], v.keys, grp="kv%d" % st)
            v = V(Vh[st].ap[:, q4 * 16:(q4 + 1) * 16, :], Vh[st].kr(q4 * 2048, 2048))
            c.dma("sp", v.ap, vall_d.ap()[h, :, q4 * 2048:(q4 + 1) * 2048].rearrange("p (c d) -> p c d", d=128), [], v.keys, grp="kv%d" % st)
    GEN = 7

    def qprep(h):
        st = h % 2
        for hf in range(2):
            o = c.PS(GEN)
            for k in range(4):
                c.mm(o, V(wuq_t[:, k, h * 192:h * 192 + 128], CK), V(cqn_t[:, k, hf * 512:(hf + 1) * 512], CK), k == 0, k == 3)
            c.copy(V(qn_t[:, st, hf * 512:(hf + 1) * 512], [("qn", st, hf)]), o)
            o = c.PSs(GEN, 0, 64, 0, 512)
            for k in range(4):
                c.mm(o, V(wuq_t[:, k, h * 192 + 128:h * 192 + 192], CK), V(cqn_t[:, k, hf * 512:(hf + 1) * 512], CK), k == 0, k == 3)
            qb = V(c.qrb.ap[0:64, :], c.qrb.kr(0, 512))
            t1 = V(c.tmp.ap[0:64, 0, :], c.tmp.kr(0, 512))
            t2 = V(c.tmp.ap[0:64, 1, :], c.tmp.kr(512, 512))
            c.copy(qb, o)
            c.tt(t1, o, V(cs_t[:, hf * 512:(hf + 1) * 512], CK), ALU.mult)
            o2 = c.PSs(GEN, 0, 64, 0, 512)
            c.mm(o2, c.rmat, qb, True, True)
            c.tt(t2, o2, V(sn_t[:, hf * 512:(hf + 1) * 512], CK), ALU.mult)
            c.tt(V(qr_t[:, st, hf * 512:(hf + 1) * 512], [("qr", st, hf)]), t1, t2, ALU.add)

    tiles = [(kc, hf) for kc in range(64) for hf in range(2)]

    def s_tile(h, i):
        kc, hf = tiles[i]
        st = h % 2
        b = (h * 128 + i) % 3
        o = c.PS(b)
        c.mm(o, V(KhT[st].ap[:, kc * 128:(kc + 1) * 128], KhT[st].kr(kc * 128, 128)), V(qn_t[:, st, hf * 512:(hf + 1) * 512], [("qn", st, hf)]), True, False)
        c.mm(o, V(kra_t[:, kc * 128:(kc + 1) * 128], [("kra", kc // 16)]), V(qr_t[:, st, hf * 512:(hf + 1) * 512], [("qr", st, hf)]), False, True)

    def pv_tile(h, i):
        kc, hf = tiles[i]
        st = h % 2
        b = (h * 128 + i) % 3
        p = c.Pt.v(b)
        c.act(p, c.PS(b), AF.Exp, scale=MLA_SCALE)
        c.mm(c.PS(3 + hf), V(Vh[st].ap[:, kc, :], Vh[st].kr(kc * 128, 128)), p, kc == 0, kc == 63)
        c.mm(c.PS(5 + hf), ones, p, kc == 0, kc == 63)

    load_kv(0)
    qprep(0)
    for h in range(8):
        if h + 1 < 8:
            load_kv(h + 1)
            qprep(h + 1)
        for i in range(128):
            if i == 0:
                s_tile(h, 0)
                s_tile(h, 1)
            if i + 2 < 128:
                s_tile(h, i + 2)
            pv_tile(h, i)
        for hf in range(2):
            out_y(1024 + h * 128, hf, 3 + hf, 5 + hf)
    return c.finish(), c.outs


def build_k3(final=False):
    c = Ctx(nslot=4)
    xT_d = c.din("xT", [D, T], F32)
    yT_d = c.din("yT", [D, T], BF16)
    memT_d = c.din("memT", [D, 256], F32)
    gains_d = c.din("gains", [128, NG_L + 16], F32)
    cbf_d = c.din("cbf", [128, 320], BF16)
    wd_ = {k: c.din(k, [D, D], F32) for k in ("w_out", "w_xq", "w_xk", "w_xv", "w_xo")}
    wg_d = c.din("w_gate", [D, DFF], F32)
    wu_d = c.din("w_up", [D, DFF], F32)
    wdn_d = c.din("w_down", [DFF, D], F32)
    x_o = c.dout("xo", [D, T], F32)
    X_t = c.sb("X", [128, KC, T], F32)
    A_t = c.sb("A", [128, 16384], BF16)
    B_t = c.sb("B", [128, 16384], BF16)
    C_t = c.sb("C", [128, 12288], BF16)
    gains_t = c.sb("gains_sb", [128, NG_L + 16], F32)
    hT = Tile(A_t[:, :], "A", 0, [KC, T], BF16)
    yT = Tile(B_t[:, :], "B", 0, [KC, T], BF16)
    memn = Tile(C_t[:, :], "C", 16384, [KC, 256], BF16)
    aT = Tile(B_t[:, :], "B", 0, [4, T], BF16)
    mem32 = Tile(C_t[:, :], "C", 0, [KC, 256], F32)
    kmT = Tile(C_t[:, :], "C", 0, [KC, 256], BF16)
    vm = Tile(C_t[:, :], "C", 8192, [2, D], BF16)
    qm = Tile(C_t[:, :], "C", 16384, [4, T], BF16)
    xv = _xv(X_t)
    GK = [("gains",)]
    c.dma("sp", gains_t[:, :], gains_d.ap()[:, :], [], GK, grp="cin")
    c.load_consts(cbf_d)
    ones = c.ones

    def proj_resid(W_ap, src):
        for cb in range(8):
            wt = c.wtile(W_ap, cb * 256, 256)
            for oc in range(2):
                ko = cb * 2 + oc
                for hf in range(2):
                    b = c.nxt("ps", 4)
                    for k in range(KC):
                        c.mm(c.PS(b), V(wt.ap[:, k, oc * 128:(oc + 1) * 128], wt.keys), src.v(k, (hf * 512, hf * 512 + 512)), k == 0, k == KC - 1)
                    c.tt(xv(ko, hf), xv(ko, hf), c.PS(b), ALU.add)

    for k in range(KC):
        v = mem32.v(k)
        c.dma("sp", v.ap, memT_d.ap()[k * 128:(k + 1) * 128, :], [], v.keys, grp="memin")
    r = c.rms_stats(lambda k, hf_: mem32.v(k), KC, D, 0, width=256)
    for k in range(KC):
        c.stt(memn.v(k), mem32.v(k), gains_t[:, G_MEMN + k:G_MEMN + k + 1], r, ALU.mult, ALU.mult, extra=GK)
    for cb in range(8):
        wt = c.wtile(wd_["w_xk"].ap(), cb * 256, 256)
        for oc in range(2):
            o = c.PSs(c.nxt("ps", 4), 0, 128, 0, 256)
            for k in range(KC):
                c.mm(o, V(wt.ap[:, k, oc * 128:(oc + 1) * 128], wt.keys), memn.v(k), k == 0, k == KC - 1)
            c.copy(kmT.v(cb * 2 + oc), o)
    for cb in range(8):
        wt = c.wtile(wd_["w_xv"].ap(), cb * 256, 256)
        for mc in range(2):
            o = c.PSs(c.nxt("ps", 4), 0, 128, 0, 256)
            for k in range(KC):
                c.mm(o, V(memn.ap[:, k, mc * 128:(mc + 1) * 128], memn.kr(k * 256 + mc * 128, 128)), V(wt.ap[:, k, :], wt.keys), k == 0, k == KC - 1)
            c.copy(vm.v(mc, (cb * 256, cb * 256 + 256)), o)
    for k in range(KC):
        v = yT.v(k)
        c.dma("act", v.ap, yT_d.ap()[k * 128:(k + 1) * 128, :], [], v.keys, grp="yin")
    _load_x(c, X_t, xT_d)
    for hf in range(2):
        for g0, gofs in ((0, G_NAO), (8, G_MLAO)):
            r = c.rms_stats(lambda k, hf_, g0=g0: yT.v(g0 + k, (hf_ * 512, hf_ * 512 + 512)), 8, 1024, hf)
            for k in range(8):
                vv = yT.v(g0 + k, (hf * 512, hf * 512 + 512))
                c.stt(vv, vv, gains_t[:, gofs + k:gofs + k + 1], r, ALU.mult, ALU.mult, extra=GK)
    proj_resid(wd_["w_out"].ap(), yT)
    _rmsnorm_x(c, xv, gains_t, G_MEM, hT)
    for hd in range(4):
        for cbq in range(2):
            wt = c.wtile(wd_["w_xq"].ap(), hd * 512 + cbq * 256, 256)
            for oc in range(2):
                dc = cbq * 2 + oc
                for hf in range(2):
                    b = c.nxt("ps", 4)
                    for k in range(KC):
                        c.mm(c.PS(b), V(wt.ap[:, k, oc * 128:(oc + 1) * 128], wt.keys), hT.v(k, (hf * 512, hf * 512 + 512)), k == 0, k == KC - 1)
                    c.copy(qm.v(dc, (hf * 512, hf * 512 + 512)), c.PS(b))
        for hf in range(2):
            pts = []
            for mc in range(2):
                b = 4 + mc
                for dc in range(4):
                    c.mm(c.PS(b), V(kmT.ap[:, hd * 4 + dc, mc * 128:(mc + 1) * 128], kmT.kr((hd * 4 + dc) * 256 + mc * 128, 128)),
                         qm.v(dc, (hf * 512, hf * 512 + 512)), dc == 0, dc == 3)
                p = c.Pt.v(c.nxt("pt", 3))
                c.act(p, c.PS(b), AF.Exp, scale=X_SCALE)
                pts.append(p)
            for mc in range(2):
                c.mm(c.PS(6), ones, pts[mc], mc == 0, mc == 1)
            r = c.rr.v(c.nxt("rr", 2))
            c.recip(r, c.PS(6))
            for dc in range(4):
                b = c.nxt("ps", 4)
                for mc in range(2):
                    f0 = (hd * 4 + dc) * 128
                    c.mm(c.PS(b), V(vm.ap[:, mc, f0:f0 + 128], vm.kr(mc * D + f0, 128)), pts[mc], mc == 0, mc == 1)
                c.tt(yT.v(hd * 4 + dc, (hf * 512, hf * 512 + 512)), c.PS(b), r, ALU.mult)
    proj_resid(wd_["w_xo"].ap(), yT)
    _rmsnorm_x(c, xv, gains_t, G_FFN, hT)
    for fb in range(11):
        for cbf_ in range(2):
            wg = c.wtile(wg_d.ap(), fb * 512 + cbf_ * 256, 256)
            wu = c.wtile(wu_d.ap(), fb * 512 + cbf_ * 256, 256)
            for oc in range(2):
                fc = cbf_ * 2 + oc
                for hf in range(2):
                    bg = 2 * c.nxt("gu", 2)
                    for k in range(KC):
                        c.mm(c.PS(bg), V(wg.ap[:, k, oc * 128:(oc + 1) * 128], wg.keys), hT.v(k, (hf * 512, hf * 512 + 512)), k == 0, k == KC - 1)
                    for k in range(KC):
                        c.mm(c.PS(bg + 1), V(wu.ap[:, k, oc * 128:(oc + 1) * 128], wu.keys), hT.v(k, (hf * 512, hf * 512 + 512)), k == 0, k == KC - 1)
                    sg = c.tmp.v(c.nxt("tmp", 2))
                    c.act(sg, c.PS(bg), AF.Silu)
                    c.tt(aT.v(fc, (hf * 512, hf * 512 + 512)), sg, c.PS(bg + 1), ALU.mult)
        for dh in range(2):
            wt = c.wtile(wdn_d.ap()[fb * 512:(fb + 1) * 512, :], dh * 1024, 1024, kc=4)
            for dk in range(8):
                ko = dh * 8 + dk
                for hf in range(2):
                    b = 4 + c.nxt("dn", 4)
                    for fc in range(4):
                        c.mm(c.PS(b), V(wt.ap[:, fc, dk * 128:(dk + 1) * 128], wt.keys), aT.v(fc, (hf * 512, hf * 512 + 512)), fc == 0, fc == 3)
                    c.tt(xv(ko, hf), xv(ko, hf), c.PS(b), ALU.add)
                if fb == 10 and not final:
                    c.dma("sp", x_o.ap()[ko * 128:(ko + 1) * 128, :], X_t[:, ko, :], [("X", ko, 0), ("X", ko, 1)], [("out", ko)], grp="xo")
    if final:
        for hf in range(2):
            r = c.rms_stats(xv, KC, D, hf)
            for k in range(KC):
                t = c.tmp.v(c.nxt("tmp", 2))
                c.stt(t, xv(k, hf), gains_t[:, NG_L + k:NG_L + k + 1], r, ALU.mult, ALU.mult, extra=GK)
                c.dma("sp", x_o.ap()[k * 128:(k + 1) * 128, hf * 512:(hf + 1) * 512], t.ap, t.keys, [("out", k, hf)], grp="xo")
    return c.finish(), c.outs


def _bf(a):
    return np.ascontiguousarray(a).astype(ml_dtypes.bfloat16)


_PROGS = {}


def _prog(name):
    if name not in _PROGS:
        _PROGS[name] = {"k1": build_k1, "k2": build_k2, "k3": lambda: build_k3(False), "k3f": lambda: build_k3(True)}[name]()
    return _PROGS[name]


def _consts():
    cbf = np.zeros((128, 320), np.float32)
    cbf[:, 0:128] = 1.0
    cbf[:, 128:256] = np.eye(128, dtype=np.float32)
    for m in range(32):
        cbf[m + 32, 256 + m] = -1.0
        cbf[m, 256 + 32 + m] = 1.0
    ind2 = np.zeros((2, 128), np.float32)
    ind2[0, 0:64] = 1.0
    ind2[1, 64:128] = 1.0
    inv = (1.0 / (10000.0 ** (np.arange(0, 64, 2, dtype=np.float32) / np.float32(64)))).astype(np.float32)
    cs, sn, rms = [], [], []
    for c in range(NC_):
        pos = (c * T + np.arange(T)).astype(np.float32)
        ang = (pos[:, None] * inv[None, :]).astype(np.float32)
        cs.append(np.concatenate([np.cos(ang).T, np.cos(ang).T], 0).astype(np.float32))
        sn.append(np.concatenate([np.sin(ang).T, np.sin(ang).T], 0).astype(np.float32))
        R0 = 16 * c
        rm = np.zeros((2, 56, 2, 64), np.float32)
        for b in range(8):
            for jj in range(7):
                for a in range(2):
                    for e in range(2):
                        kr_ = R0 - 4 + 2 * (b + jj - 1) + a
                        qr_ = R0 + 2 * b + e
                        rs_ = min(max(qr_ - 4, 0), 120)
                        ok = (0 <= kr_ <= 127) and (rs_ <= kr_ < rs_ + 8)
                        rm[a, b * 7 + jj, e, :] = 0.0 if ok else NEG
        rms.append(_bf(rm.reshape(2, 7168)))
    return _bf(cbf), _bf(ind2), cs, sn, rms


def _tt_tiles(rpb_l):
    kc = np.arange(64)
    col_start = np.clip(kc - 8, 0, 48)
    colvalid = (kc[:, None] >= col_start[None, :]) & (kc[:, None] < col_start[None, :] + 16)
    dcc = np.clip(kc[:, None] - kc[None, :] + 15, 0, 30)
    tt = np.full((8, 128, 7 * 128), NEG, np.float32)
    for jj in range(7):
        for a in range(2):
            for e in range(2):
                dr = 2 * (jj - 1) + a - e - 4 + 7
                if 0 <= dr <= 14:
                    blk = rpb_l[:, dr][:, dcc]
                    blk = np.where(colvalid[None], blk, np.float32(NEG))
                    tt[:, a * 64:(a + 1) * 64, jj * 128 + e * 64: jj * 128 + (e + 1) * 64] = blk
    return tt


def _gains_layer(inp, l, with_final=False):
    f = lambda k: np.asarray(inp[k], np.float32)
    G = np.zeros((128, NG_L + (16 if with_final else 0)), np.float32)

    def put(base, vec):
        nk = vec.shape[0] // 128
        G[:, base:base + nk] = vec.reshape(nk, 128).T
    put(G_MIX, f("ln_mix")[l]); put(G_MEM, f("ln_mem")[l]); put(G_MEMN, f("mem_norm")[l]); put(G_FFN, f("ln_ffn")[l])
    put(G_QN, f("q_norm")[l]); put(G_KVN, f("kv_norm")[l]); put(G_NAO, f("na_out_norm")[l]); put(G_MLAO, f("mla_out_norm")[l])
    if with_final:
        put(NG_L, f("final_norm"))
    return G


def _launch(name, maps):
    nc, outs = _prog(name)
    res = run_bass_kernel_spmd(nc, maps, core_ids=list(range(NC_)))
    return [{k: np.asarray(r[k]) for k in outs} for r in res.results]


def run_layer(inp, l, xTs, cst, last, upto=3):
    cbf, ind2, cs, sn, rms = cst
    f = lambda k: np.ascontiguousarray(np.asarray(inp[k], np.float32)[l])
    G1 = _gains_layer(inp, l)
    w_in = f("w_in")
    w_uq, w_ukv = f("w_uq"), f("w_ukv")
    r1 = _launch("k1", [{"xT": xTs[c], "w_in": w_in, "w_ukv": w_ukv, "gains": G1, "cs": cs[c], "sn": sn[c], "cbf": cbf} for c in range(NC_)])
    if upto == 1:
        return r1
    kr_all = np.ascontiguousarray(np.concatenate([r["kr"] for r in r1], axis=1))
    k_all = np.ascontiguousarray(np.concatenate([r["kmla"] for r in r1], axis=1).reshape(8, 128, S))
    v_cat = np.concatenate([r["vmla"] for r in r1], axis=0)
    v_all = np.ascontiguousarray(v_cat.reshape(64, 128, 8, 128).transpose(2, 1, 0, 3).reshape(8, 128, S))
    tt = _tt_tiles(np.asarray(inp["na_rpb"], np.float32)[l])
    maps2 = []
    for c in range(NC_):
        p, n = max(c - 1, 0), min(c + 1, NC_ - 1)
        knah = np.concatenate([r1[p]["kna"][:, 768:], r1[c]["kna"], r1[n]["kna"][:, :256]], axis=1)
        vnah = np.concatenate([r1[p]["vna"][768:], r1[c]["vna"], r1[n]["vna"][:256]], axis=0)
        maps2.append({"qna": r1[c]["qna"], "knah": np.ascontiguousarray(knah), "vnah": np.ascontiguousarray(vnah),
                      "cqn": r1[c]["cqn"], "k_all": k_all, "v_all": v_all, "kr_all": kr_all, "w_uq": w_uq, "tt": tt,
                      "rms": rms[c], "ind2": ind2, "cs": cs[c], "sn": sn[c], "cbf": cbf})
    r2 = _launch("k2", maps2)
    if upto == 2:
        return r2
    G3 = _gains_layer(inp, l, with_final=True)
    memT = np.ascontiguousarray(np.asarray(inp["mem"], np.float32)[0].T)
    wk = {k: f(k) for k in ("w_out", "w_xq", "w_xk", "w_xv", "w_xo", "w_gate", "w_up", "w_down")}
    maps3 = [dict(wk, xT=xTs[c], yT=r2[c]["yT"], memT=memT, gains=G3, cbf=cbf) for c in range(NC_)]
    r3 = _launch("k3f" if last else "k3", maps3)
    return [r["xo"] for r in r3]


def kernel(**inputs):
    x = np.asarray(inputs["x"], np.float32)[0]
    cst = _consts()
    xTs = [np.ascontiguousarray(x[c * T:(c + 1) * T].T) for c in range(NC_)]
    for l in range(L):
        xTs = run_layer(inputs, l, xTs, cst, last=(l == L - 1))
    out = np.concatenate([a.T for a in xTs], axis=0)
    return np.ascontiguousarray(out.reshape(1, S, D).astype(np.float32))
```

```python
import numpy as np
import ml_dtypes
from contextlib import ExitStack
import concourse.bass as bass
import concourse.mybir as mybir
from concourse.bass_utils import run_bass_kernel_spmd

F32 = mybir.dt.float32
BF16 = mybir.dt.bfloat16
I32 = mybir.dt.int32
AF = mybir.ActivationFunctionType
ALU = mybir.AluOpType

NC_ = 8
D = 2048
S = 8192
T = 1024
L = 4
KC = 16
DFF = 5632
INW = 3904
EPS = 1e-6
NA_SCALE = 128 ** -0.5
MLA_SCALE = 192 ** -0.5
X_SCALE = 512 ** -0.5
NEG = -1.0e5

SZA = 256 * INW + 512 * 192 + 256 * 256
SZB = 5 * 256 * 2048
SZC = 2 * 256 * DFF + DFF * 256
OFF_WIN, OFF_WUQ, OFF_WUKV = 0, 256 * INW, 256 * INW + 512 * 192
OFF_WOUT, OFF_WXQ, OFF_WXK, OFF_WXV, OFF_WXO = [i * 256 * 2048 for i in range(5)]
OFF_WG, OFF_WU, OFF_WD = 0, 256 * DFF, 2 * 256 * DFF
X_CKV, X_KR, X_KNA, X_VNA = 0, 256 * 1024, 320 * 1024, 320 * 1024 + 8 * 128 * 512
SZX = X_VNA + 512 * 1024
NG_L = 86
G_MIX, G_MEM, G_MEMN, G_FFN, G_QN, G_KVN, G_NAO, G_MLAO = 0, 16, 32, 48, 64, 68, 70, 78
NG = L * NG_L + 16
GR = 256


class Op:
    __slots__ = ("eng", "fn", "deps", "grp", "amt", "idx", "needs_inc", "cwaits", "dwaits",
                 "dma_waits", "count", "epoch", "vc")


class Prog:
    CE = ("pe", "act", "dve", "pool", "sp")
    EI = {"pe": 0, "act": 1, "dve": 2, "pool": 3, "sp": 4}

    def __init__(self):
        self.q = {e: [] for e in self.CE}
        self.res = {}
        self.all = []
        self.epoch = 0
        self.grp_total = {}

    def add(self, eng, fn, reads=(), writes=(), grp=None, amt=16):
        psr = [k for k in reads if k[0] == "ps"]
        if psr:
            reads = [k for k in reads if k[0] != "ps"]
            writes = list(writes) + psr
        op = Op()
        op.eng, op.fn, op.grp, op.amt, op.epoch = eng, fn, grp, amt, self.epoch
        op.needs_inc = False
        deps = []
        for k in reads:
            st = self.res.get(k)
            if st is not None and st[0] is not None:
                deps.append(st[0])
        for k in writes:
            st = self.res.get(k)
            if st is not None:
                if st[0] is not None:
                    deps.append(st[0])
                deps.extend(st[1].values())
                deps.extend(st[2])
        dmaw = {}
        cdeps = []
        for d in deps:
            if d is op:
                continue
            if d.grp is not None:
                dmaw[d.grp] = self.grp_total[d.grp]
            else:
                cdeps.append(d)
        op.deps = cdeps
        op.dma_waits = dmaw
        for k in reads:
            st = self.res.get(k)
            if st is None:
                st = self.res[k] = [None, {}, []]
            if grp is None:
                st[1][eng] = op
            else:
                st[2].append(op)
        for k in writes:
            self.res[k] = [op, {}, []]
        self.all.append(op)
        op.idx = len(self.q[eng])
        self.q[eng].append(op)
        if grp is not None:
            self.grp_total[grp] = self.grp_total.get(grp, 0) + amt
        return op

    def finalize(self):
        known = {e: [-1] * 5 for e in self.CE}
        dknown = {e: {} for e in self.CE}
        for op in self.all:
            E = op.eng
            kn = known[E]
            cw = []
            for d in op.deps:
                if d.eng == "pe" and E == "pe":
                    continue
                di = self.EI[d.eng]
                if kn[di] >= d.idx:
                    continue
                cw.append(d)
                d.needs_inc = True
                vc = d.vc
                for i in range(5):
                    if vc[i] > kn[i]:
                        kn[i] = vc[i]
            op.cwaits = cw
            dw = {}
            dk = dknown[E]
            for g, v in op.dma_waits.items():
                if dk.get(g, 0) >= v:
                    continue
                dk[g] = v
                dw[g] = v
            op.dwaits = dw
            if op.grp is None:
                vc = list(kn)
                ei = self.EI[E]
                if op.idx > vc[ei]:
                    vc[ei] = op.idx
                op.vc = tuple(vc)
            else:
                op.vc = None
        cnt = {}
        for e in self.CE:
            for op in self.q[e]:
                if op.grp is None and op.needs_inc:
                    key = (e, op.epoch)
                    cnt[key] = cnt.get(key, 0) + 1
                    op.count = cnt[key]
        return cnt

    def emit(self, nc, es):
        cnt = self.finalize()
        csem = {}
        for key in cnt:
            csem[key] = es.enter_context(nc.semaphore("c_%s_%d" % key))
        dsem = {}
        for g in self.grp_total:
            dsem[g] = es.enter_context(nc.semaphore("d_%s" % (str(g).replace(" ", ""))))
        self.nsem = len(csem) + len(dsem)
        block = es.enter_context(nc.Block())

        def run(eng_name):
            def body(eng):
                for op in self.q[eng_name]:
                    for d in op.cwaits:
                        eng.wait_ge(csem[(d.eng, d.epoch)], d.count)
                    for g, v in op.dwaits.items():
                        eng.wait_ge(dsem[g], v)
                    if op.fn is None:
                        continue
                    ins = op.fn(eng)
                    if op.grp is not None:
                        ins.then_inc(dsem[op.grp], op.amt)
                    elif op.needs_inc:
                        ins.then_inc(csem[(eng_name, op.epoch)], 1)
            return body

        block.tensor(run("pe"))
        block.scalar(run("act"))
        block.vector(run("dve"))
        block.gpsimd(run("pool"))
        block.sync(run("sp"))


class V:
    __slots__ = ("ap", "keys")

    def __init__(self, ap, keys):
        self.ap, self.keys = ap, keys


class Tile:
    def __init__(self, arena_ap, arena_name, off, shape, dt, parts=128):
        self.esz = 4 if dt in (F32, I32) else 2
        n = int(np.prod(shape))
        nb = n * self.esz
        assert off % 4 == 0 and off + nb <= arena_ap.shape[1] * 2, (arena_name, off, nb)
        a = arena_ap[0:parts, off // 2: off // 2 + nb // 2]
        if self.esz == 4:
            a = a.bitcast(dt)
        if len(shape) == 2:
            a = a.rearrange("p (a b) -> p a b", a=shape[0])
        elif len(shape) == 3:
            a = a.rearrange("p (a b c) -> p a b c", a=shape[0], b=shape[1])
        self.ap, self.off, self.shape, self.arena = a, off, list(shape), arena_name

    def kr(self, lo, n):
        b0 = self.off + lo * self.esz
        b1 = b0 + n * self.esz
        return [(self.arena, g) for g in range(b0 // GR, (b1 - 1) // GR + 1)]

    def v(self, *idx):
        sh = self.shape
        ints = [i for i in idx if not isinstance(i, tuple)]
        rng = [i for i in idx if isinstance(i, tuple)]
        ap = self.ap
        sl = [slice(None)] + list(ints)
        lo_e = 0
        stride = int(np.prod(sh))
        for d, i in enumerate(ints):
            stride //= sh[d]
            lo_e += i * stride
        n = stride
        if rng:
            lo, hi = rng[0]
            sl.append(slice(lo, hi))
            inner = stride // sh[len(ints)]
            lo_e += lo * inner
            n = (hi - lo) * inner
        ap = ap[tuple(sl)]
        return V(ap, self.kr(lo_e, n))


class Ctx:
    def __init__(self, nslot=4):
        self.nc = nc = bass.Bass("TRN2", target_bir_lowering=False)
        self.P = Prog()
        self.es = ExitStack()
        self.rot = {}
        self.outs = []
        self.NSLOT = nslot
        self.wn = 0
        self.RING = self.sb("RING", [128, nslot * 4096], BF16)
        self.M = self.sb("M", [128, 8192], BF16)
        self.ps = [self.es.enter_context(nc.psum_tensor("ps%d" % i, [128, 512], F32)) for i in range(8)]
        M_ap = self.M[:, :]
        self.stg = Tile(M_ap, "M", 0, [2, 512], BF16)
        self.sq = Tile(M_ap, "M", 2048, [2, 512], BF16)
        self.Pt = Tile(M_ap, "M", 4096, [3, 512], BF16)
        self.tmp = Tile(M_ap, "M", 7168, [2, 512], F32)
        self.rr = Tile(M_ap, "M", 11264, [2, 512], F32)
        self.qrb = Tile(M_ap, "M", 15360, [512], BF16)
        self.cbf_t = self.sb("cbf_sb", [128, 320], BF16)
        self.ones = V(self.cbf_t[:, 0:128], [("cbf",)])
        self.ident = V(self.cbf_t[:, 128:256], [("cbf",)])
        self.rmat = V(self.cbf_t[0:64, 256:320], [("cbf",)])

    def din(self, name, shape, dt):
        return self.nc.dram_tensor(name, shape, dt, kind="ExternalInput")

    def dout(self, name, shape, dt):
        t = self.nc.dram_tensor(name, shape, dt, kind="ExternalOutput")
        self.outs.append(name)
        return t

    def sb(self, name, shape, dt):
        return self.es.enter_context(self.nc.sbuf_tensor(name, shape, dt))

    def nxt(self, name, n):
        self.rot[name] = (self.rot.get(name, -1) + 1) % n
        return self.rot[name]

    def PS(self, b):
        return V(self.ps[b][:, :], [("ps", b)])

    def PSs(self, b, p0, p1, c0, c1):
        return V(self.ps[b][p0:p1, c0:c1], [("ps", b)])

    def mm(self, out, lhsT, rhs, start, stop):
        self.P.add("pe", lambda e, o=out.ap, a=lhsT.ap, b=rhs.ap, s=start, t=stop: e.matmul(o, a, b, start=s, stop=t),
                   reads=lhsT.keys + rhs.keys, writes=out.keys)

    def act(self, out, in_, func, scale=1.0):
        self.P.add("act", lambda e, o=out.ap, i=in_.ap, f=func, s=scale: e.activation(o, i, f, scale=s),
                   reads=in_.keys, writes=out.keys)

    def tt(self, out, a, b, op, eng="dve"):
        self.P.add(eng, lambda e, o=out.ap, x=a.ap, y=b.ap, p=op: e.tensor_tensor(o, x, y, p),
                   reads=a.keys + b.keys, writes=out.keys)

    def ts(self, out, a, s1, s2, op0, op1, extra=()):
        self.P.add("dve", lambda e, o=out.ap, x=a.ap, u=s1, w=s2, p=op0, q=op1: e.tensor_scalar(o, x, u, w, p, q),
                   reads=a.keys + list(extra), writes=out.keys)

    def stt(self, out, a, scalar, b, op0, op1, extra=()):
        self.P.add("dve", lambda e, o=out.ap, x=a.ap, s=scalar, y=b.ap, p=op0, q=op1: e.scalar_tensor_tensor(o, x, s, y, p, q),
                   reads=a.keys + b.keys + list(extra), writes=out.keys)

    def copy(self, out, a, eng="dve"):
        self.P.add(eng, lambda e, o=out.ap, x=a.ap: e.tensor_copy(o, x), reads=a.keys, writes=out.keys)

    def recip(self, out, a):
        self.P.add("dve", lambda e, o=out.ap, x=a.ap: e.reciprocal(o, x), reads=a.keys, writes=out.keys)

    def dma(self, eng, out_ap, in_ap, reads, writes, grp):
        self.P.add(eng, lambda e, o=out_ap, i=in_ap: e.dma_start(out=o, in_=i), reads=reads, writes=writes, grp=grp)

    def load_consts(self, cbf_d):
        self.dma("sp", self.cbf_t[:, :], cbf_d.ap()[:, :], [], [("cbf",)], grp="cin")

    def rstd_from(self, r, acc, Dn):
        self.ts(r, acc, 1.0 / Dn, EPS, ALU.mult, ALU.add)
        self.act(r, r, AF.Sqrt)
        self.recip(r, r)

    def rms_stats(self, src_fn, nk, Dn, hf, width=512):
        b = self.nxt("ps", 2)
        acc = self.PSs(b, 0, 128, 0, width)
        for k in range(nk):
            s0 = self.sq.v(self.nxt("sq", 2))
            s = V(s0.ap[:, 0:width], s0.keys)
            self.act(s, src_fn(k, hf), AF.Square)
            self.mm(acc, self.ones, s, k == 0, k == nk - 1)
        r0 = self.rr.v(self.nxt("rr", 2))
        r = V(r0.ap[:, 0:width], r0.keys)
        self.rstd_from(r, acc, Dn)
        return r

    def wtile(self, W_ap, c0, n, kc=16):
        s = self.wn % self.NSLOT
        self.wn += 1
        base = self.RING[:, s * 4096: s * 4096 + kc * n]
        dst = base.rearrange("p (k c) -> p k c", k=kc)
        src = W_ap[:, c0:c0 + n].rearrange("(k p) c -> p k c", p=128)
        keys = [("RING", s)]
        self.dma("pool", dst, src, [], keys, grp="ring%d" % s)
        return V(dst, keys)

    def finish(self):
        P = self.P
        P.add("sp", None, reads=[k for k in P.res if k[0] == "out"], writes=[])
        P.emit(self.nc, self.es)
        self.es.close()
        return self.nc


def _xv(X_t):
    return lambda k, hf: V(X_t[:, k, hf * 512:(hf + 1) * 512], [("X", k, hf)])


def _load_x(c, X_t, xT_d):
    for hf in range(2):
        for k in range(KC):
            q = ("sp", "act")[k % 2]
            c.dma(q, X_t[:, k, hf * 512:(hf + 1) * 512], xT_d.ap()[k * 128:(k + 1) * 128, hf * 512:(hf + 1) * 512], [],
                  [("X", k, hf)], grp="xin%d" % (k % 2))


def _rmsnorm_x(c, xv, gains_t, gbase, dst, gk=(("gains",),)):
    for hf in range(2):
        r = c.rms_stats(xv, KC, D, hf)
        for k in range(KC):
            c.stt(dst.v(k, (hf * 512, hf * 512 + 512)), xv(k, hf), gains_t[:, gbase + k:gbase + k + 1], r,
                  ALU.mult, ALU.mult, extra=list(gk))


def build_k1(c=None, X_t=None, A_t=None, B_t=None, C_t=None):
    chained = c is not None
    if not chained:
        c = Ctx(nslot=4)
        xT_d = c.din("xT", [D, T], F32)
    w_in = c.din("w_in", [D, INW], F32)
    gains_d = c.din("gains1" if chained else "gains", [128, NG_L], F32)
    cs_d = c.din("cs", [64, T], F32)
    sn_d = c.din("sn", [64, T], F32)
    if not chained:
        cbf_d = c.din("cbf", [128, 320], BF16)
    wukv_d = c.din("w_ukv", [256, 2048], F32)
    kmla_o = c.dout("kmla", [1024, T], BF16)
    vmla_o = c.dout("vmla", [T, 1024], BF16)
    qna_o = c.dout("qna", [1024, T], BF16)
    kna_o = c.dout("kna", [1024, T], BF16)
    vna_o = c.dout("vna", [T, 1024], BF16)
    cqn_o = c.dout("cqn", [512, T], BF16)
    ckvn_o = c.dout("ckvn", [256, T], BF16)
    kr_o = c.dout("kr", [64, T], BF16)
    gains_t = c.sb("gains1_sb", [128, NG_L], F32)
    GK1 = [("gains1",)]
    if not chained:
        X_t = c.sb("X", [128, KC, T], F32)
        A_t = c.sb("A", [128, 16384], BF16)
        C_t = c.sb("C", [128, 8192], BF16)
        wukv_t = c.sb("wukv_sb", [128, 2, 2048], BF16)
        ostg_t = c.sb("ostg_sb", [128, 4, 1024], BF16)
        WK = [("wukv",)]
        okeys = lambda i, hf: [("ostg", i, hf)]
    else:
        wk_tile = Tile(B_t[:, :], "B", 16384, [2, 2048], BF16)
        os_tile = Tile(B_t[:, :], "B", 24576, [4, 1024], BF16)
        wukv_t, ostg_t = wk_tile.ap, os_tile.ap
        WK = wk_tile.kr(0, 4096)
        okeys = lambda i, hf: os_tile.kr(i * 1024 + hf * 512, 512)
    c.dma("pool", wukv_t[:, :, :], wukv_d.ap().rearrange("(k p) c -> p k c", p=128), [], WK, grp="cinw")

    def ostg(hf):
        i = c.rot.get("os", 0)
        if hf == 0:
            i = c.nxt("os", 4)
        return i, V(ostg_t[:, i, hf * 512:(hf + 1) * 512], okeys(i, hf))

    def oq():
        return ("sp", "act")[c.nxt("oq", 2)]

    hT = Tile(A_t[:, :], "A", 0, [KC, T], BF16)
    cq16 = Tile(C_t[:, :], "C", 0, [4, T], BF16)
    ckv16 = Tile(C_t[:, :], "C", 8192, [2, T], BF16)
    kr32 = Tile(C_t[:, :], "C", 12288, [T], F32, parts=64)
    cst = Tile(c.M[:, :], "M", 0, [512], F32)
    snt = Tile(c.M[:, :], "M", 2048, [512], F32)
    xv = _xv(X_t)
    if not chained:
        _load_x(c, X_t, xT_d)
        c.load_consts(cbf_d)
    c.dma("sp", gains_t[:, :], gains_d.ap()[:, :], [], GK1, grp="cin1")
    _rmsnorm_x(c, xv, gains_t, G_MIX, hT, gk=GK1)
    W = w_in.ap()

    def fm_group(wt, oc, hf, M=128):
        b = c.nxt("ps", 2)
        o = c.PSs(b, 0, M, 0, 512)
        for k in range(KC):
            c.mm(o, V(wt.ap[:, k, oc * 128: oc * 128 + M], wt.keys), hT.v(k, (hf * 512, hf * 512 + 512)), k == 0, k == KC - 1)
        return o

    for (c0, n) in ((3072, 256), (3328, 256), (3584, 256), (3840, 64)):
        wt = c.wtile(W, c0, n)
        for oc in range((n + 127) // 128):
            M = min(128, n - oc * 128)
            for hf in range(2):
                o = fm_group(wt, oc, hf, M)
                gi = (c0 - 3072) // 128 + oc
                if gi < 6:
                    s = c.sq.v(c.nxt("sq", 2))
                    c.act(s, o, AF.Square)
                    iscq = gi < 4
                    nk, kk = (4, gi) if iscq else (2, gi - 4)
                    bb = (2 if iscq else 4) + hf
                    c.mm(c.PS(bb), c.ones, s, kk == 0, kk == nk - 1)
                    dst = cq16.v(gi, (hf * 512, hf * 512 + 512)) if iscq else ckv16.v(gi - 4, (hf * 512, hf * 512 + 512))
                    c.copy(dst, o)
                else:
                    c.copy(V(kr32.ap[:, hf * 512:(hf + 1) * 512], kr32.kr(hf * 512, 512)), o)
    for hf in range(2):
        for t16, nk, Dn, gofs, bb, od in ((cq16, 4, 512, G_QN, 2 + hf, cqn_o), (ckv16, 2, 256, G_KVN, 4 + hf, ckvn_o)):
            r = c.rr.v(c.nxt("rr", 2))
            c.rstd_from(r, c.PS(bb), Dn)
            for k in range(nk):
                vv = t16.v(k, (hf * 512, hf * 512 + 512))
                c.stt(vv, vv, gains_t[:, gofs + k:gofs + k + 1], r, ALU.mult, ALU.mult, extra=GK1)
                c.dma("sp", od.ap()[k * 128:(k + 1) * 128, hf * 512:(hf + 1) * 512], vv.ap, vv.keys, [("out", od.name, k, hf)], grp="o1")
        c.dma("sp", cst.ap[0:64, :], cs_d.ap()[:, hf * 512:(hf + 1) * 512], [], cst.kr(0, 512), grp="cst")
        c.dma("sp", snt.ap[0:64, :], sn_d.ap()[:, hf * 512:(hf + 1) * 512], [], snt.kr(0, 512), grp="snt")
        krv = V(kr32.ap[:, hf * 512:(hf + 1) * 512], kr32.kr(hf * 512, 512))
        qb = V(c.qrb.ap[0:64, :], c.qrb.kr(0, 512))
        c.copy(qb, krv)
        o = c.PSs(c.nxt("ps", 2), 0, 64, 0, 512)
        c.mm(o, c.rmat, qb, True, True)
        t1 = V(c.tmp.ap[0:64, 0, :], c.tmp.kr(0, 512))
        t2 = V(c.tmp.ap[0:64, 1, :], c.tmp.kr(512, 512))
        c.tt(t1, krv, V(cst.ap[0:64, :], cst.kr(0, 512)), ALU.mult)
        c.tt(t2, o, V(snt.ap[0:64, :], snt.kr(0, 512)), ALU.mult)
        c.tt(qb, t1, t2, ALU.add)
        c.dma("sp", kr_o.ap()[:, hf * 512:(hf + 1) * 512], qb.ap, qb.keys, [("out", "kr", hf)], grp="o1")
    for base, od, scale in ((0, qna_o, NA_SCALE), (1024, kna_o, 1.0)):
        for cb in range(4):
            wt = c.wtile(W, base + cb * 256, 256)
            for oc in range(2):
                h_ = cb * 2 + oc
                for hf in range(2):
                    o = fm_group(wt, oc, hf)
                    i, s_ = ostg(hf)
                    c.act(s_, o, AF.Copy, scale=scale)
                c.dma(oq(), od.ap()[h_ * 128:(h_ + 1) * 128, :], ostg_t[:, i, :], okeys(i, 0) + okeys(i, 1),
                      [("out", od.name, h_)], grp="o2")
    for cb in range(4):
        wt = c.wtile(W, 2048 + cb * 256, 256)
        for tc in range(8):
            o = c.PSs(c.nxt("ps", 2), 0, 128, 0, 256)
            for k in range(KC):
                c.mm(o, V(hT.ap[:, k, tc * 128:(tc + 1) * 128], hT.kr(k * T + tc * 128, 128)), V(wt.ap[:, k, :], wt.keys), k == 0, k == KC - 1)
            s = c.stg.v(c.nxt("st", 2))
            sv = V(s.ap[:, 0:256], s.keys)
            c.act(sv, o, AF.Copy)
            c.dma("sp", vna_o.ap()[tc * 128:(tc + 1) * 128, cb * 256:(cb + 1) * 256], sv.ap, sv.keys, [("out", "vna", tc, cb)], grp="o3")
    for h_ in range(8):
        for hf in range(2):
            o = c.PS(c.nxt("ps6", 6))
            for k in range(2):
                c.mm(o, V(wukv_t[:, k, h_ * 256:h_ * 256 + 128], WK), ckv16.v(k, (hf * 512, hf * 512 + 512)), k == 0, k == 1)
            i, s_ = ostg(hf)
            if hf == 0:
                c.copy(s_, o)
            else:
                c.act(s_, o, AF.Copy)
        c.dma(oq(), kmla_o.ap()[h_ * 128:(h_ + 1) * 128, :], ostg_t[:, i, :], okeys(i, 0) + okeys(i, 1), [("out", "kmla", h_)], grp="o4")
    wv = wukv_t[:, :, :].rearrange("p k (h c) -> p k h c", c=256)
    for tc in range(8):
        for g in range(2):
            o = c.PS(c.nxt("ps6", 6))
            for k in range(2):
                c.mm(o, V(ckv16.ap[:, k, tc * 128:(tc + 1) * 128], ckv16.kr(k * T + tc * 128, 128)),
                     V(wv[:, k, 4 * g:4 * g + 4, 128:256], WK), k == 0, k == 1)
            i, s_ = ostg(g)
            if g == 0:
                c.copy(s_, o)
            else:
                c.act(s_, o, AF.Copy)
        c.dma(oq(), vmla_o.ap()[tc * 128:(tc + 1) * 128, :], ostg_t[:, i, :], okeys(i, 0) + okeys(i, 1), [("out", "vmla", tc)], grp="o4")
    return c.finish(), c.outs


def build_k2():
    c = Ctx(nslot=1)
    qna_d = c.din("qna", [1024, T], BF16)
    knah_d = c.din("knah", [1024, 1536], BF16)
    vnah_d = c.din("vnah", [1536, 1024], BF16)
    cqn_d = c.din("cqn", [512, T], BF16)
    kall_d = c.din("k_all", [8, 128, S], BF16)
    vall_d = c.din("v_all", [8, 128, S], BF16)
    kr_d = c.din("kr_all", [64, S], BF16)
    wuq_d = c.din("w_uq", [512, 1536], F32)
    tt_d = c.din("tt", [8, 128, 896], F32)
    rms_d = c.din("rms", [2, 7168], BF16)
    ind2_d = c.din("ind2", [2, 128], BF16)
    cs_d = c.din("cs", [64, T], F32)
    sn_d = c.din("sn", [64, T], F32)
    cbf_d = c.din("cbf", [128, 320], BF16)
    y_o = c.dout("yT", [D, T], BF16)
    U_t = c.sb("U", [128, 39936], BF16)
    cqn_t = c.sb("cqn_sb", [128, 4, T], BF16)
    wuq_t = c.sb("wuq_sb", [128, 4, 1536], BF16)
    kra_t = c.sb("kra_sb", [64, S], BF16)
    qn_t = c.sb("qn_sb", [128, 2, T], BF16)
    qr_t = c.sb("qr_sb", [64, 2, T], BF16)
    cs_t = c.sb("cs_sb", [64, T], F32)
    sn_t = c.sb("sn_sb", [64, T], F32)
    rms_t = c.sb("rms_sb", [2, 7168], BF16)
    ind2_t = c.sb("ind2_sb", [2, 128], BF16)
    U = U_t[:, :]
    qna = Tile(U, "U", 0, [8, T], BF16)
    knah = Tile(U, "U", 16384, [8, 1536], BF16)
    vnah = Tile(U, "U", 40960, [12, 1024], BF16)
    ttb = Tile(U, "U", 65536, [8, 896], BF16)
    KhT = [Tile(U, "U", st_ * 32768, [S], BF16) for st_ in range(2)]
    Vh = [Tile(U, "U", st_ * 32768 + 16384, [64, 128], BF16) for st_ in range(2)]
    CK = [("c2",)]
    c.load_consts(cbf_d)
    c.dma("sp", rms_t[:, :], rms_d.ap()[:, :], [], CK, grp="cin")
    c.dma("sp", ind2_t[:, :], ind2_d.ap()[:, :], [], CK, grp="cin")
    c.dma("sp", cs_t[:, :], cs_d.ap()[:, :], [], CK, grp="cin")
    c.dma("sp", sn_t[:, :], sn_d.ap()[:, :], [], CK, grp="cin")
    for k in range(4):
        c.dma("sp", cqn_t[:, k, :], cqn_d.ap()[k * 128:(k + 1) * 128, :], [], CK, grp="cin")
    c.dma("pool", wuq_t[:, :, :], wuq_d.ap().rearrange("(k p) c -> p k c", p=128), [], CK, grp="cinw")
    for ch in range(12):
        v = vnah.v(ch)
        c.dma(("sp", "act")[ch % 2], v.ap, vnah_d.ap()[ch * 128:(ch + 1) * 128, :], [], v.keys, grp="nain%d" % (ch % 2))
    for h in range(8):
        v = ttb.v(h)
        c.dma("pool", v.ap, tt_d.ap()[h], [], v.keys, grp="nain2")
        v = qna.v(h)
        c.dma("sp", v.ap, qna_d.ap()[h * 128:(h + 1) * 128, :], [], v.keys, grp="nain0")
        v = knah.v(h)
        c.dma("act", v.ap, knah_d.ap()[h * 128:(h + 1) * 128, :], [], v.keys, grp="nain1")
    ind2 = V(ind2_t[:, :], CK)
    ones, ident = c.ones, c.ident

    def out_y(row0, hf, O_b, den_b):
        r = c.rr.v(c.nxt("rr", 2))
        c.recip(r, c.PS(den_b))
        s = c.stg.v(c.nxt("st", 2))
        c.tt(s, c.PS(O_b), r, ALU.mult)
        c.dma("sp", y_o.ap()[row0:row0 + 128, hf * 512:(hf + 1) * 512], s.ap, s.keys, [("out", row0, hf)], grp="yo")

    for h in range(8):
        for rnd in range(2):
            O_b, den_b = 4 + (c.nxt("nao", 2)), 6 + (c.nxt("nad", 2))
            for b in range(4 * rnd, 4 * rnd + 4):
                jl = list(range(1, 6))
                if b == 0:
                    jl = jl + [6]
                if b == 7:
                    jl = [0] + jl
                SA, SB = c.nxt("sa", 2), 2 + c.nxt("sb", 2)
                slot = []
                for si, jj in enumerate(jl):
                    ch = b + jj - 1
                    bank, col = (SA, si * 128) if si < 4 else (SB, (si - 4) * 128)
                    slot.append((bank, col))
                    o = c.PSs(bank, 0, 128, col, col + 128)
                    kk = V(knah.ap[:, h, ch * 128:(ch + 1) * 128], knah.kr(h * 1536 + ch * 128, 128))
                    qq = V(qna.ap[:, h, b * 128:(b + 1) * 128], qna.kr(h * T + b * 128, 128))
                    c.mm(o, kk, qq, True, False)
                    c.mm(o, ident, V(ttb.ap[:, h, jj * 128:(jj + 1) * 128], ttb.kr(h * 896 + jj * 128, 128)), False, False)
                    c.mm(o, ind2, V(rms_t[:, (b * 7 + jj) * 128:(b * 7 + jj + 1) * 128], CK), False, True)
                nB = len(jl) - 4
                pa = c.Pt.v(c.nxt("pt", 3))
                c.act(pa, c.PS(SA), AF.Exp)
                pb0 = c.Pt.v(c.nxt("pt", 3))
                pb = V(pb0.ap[:, 0:nB * 128], pb0.keys)
                c.act(pb, c.PSs(SB, 0, 128, 0, nB * 128), AF.Exp)
                col = (b % 4) * 128
                pvs = [V(pa.ap[:, si * 128:(si + 1) * 128], pa.keys) if si < 4 else V(pb0.ap[:, (si - 4) * 128:(si - 3) * 128], pb0.keys)
                       for si in range(len(jl))]
                for si, jj in enumerate(jl):
                    ch = b + jj - 1
                    vv = V(vnah.ap[:, ch, h * 128:(h + 1) * 128], vnah.kr(ch * 1024 + h * 128, 128))
                    c.mm(c.PSs(O_b, 0, 128, col, col + 128), vv, pvs[si], si == 0, si == len(jl) - 1)
                for si, jj in enumerate(jl):
                    c.mm(c.PSs(den_b, 0, 128, col, col + 128), ones, pvs[si], si == 0, si == len(jl) - 1)
            out_y(h * 128, rnd, O_b, den_b)

    for q4 in range(4):
        c.dma("sp", kra_t[:, q4 * 2048:(q4 + 1) * 2048], kr_d.ap()[:, q4 * 2048:(q4 + 1) * 2048], [], [("kra", q4)], grp="mlain")

    def load_kv(h):
        st = h % 2
        for q4 in range(4):
            v = V(KhT[st].ap[:, q4 * 2048:(q4 + 1) * 2048], KhT[st].kr(q4 * 2048, 2048))
            c.dma("sp", v.ap, kall_d.ap()[h, :, q4 * 2048:(q4 + 1) * 2048], [], v.keys, grp="kv%d" % st)
            v = V(Vh[st].ap[:, q4 * 16:(q4 + 1) * 16, :], Vh[st].kr(q4 * 2048, 2048))
            c.dma("sp", v.ap, vall_d.ap()[h, :, q4 * 2048:(q4 + 1) * 2048].rearrange("p (c d) -> p c d", d=128), [], v.keys, grp="kv%d" % st)
    GEN = 7

    def qprep(h):
        st = h % 2
        for hf in range(2):
            o = c.PS(GEN)
            for k in range(4):
                c.mm(o, V(wuq_t[:, k, h * 192:h * 192 + 128], CK), V(cqn_t[:, k, hf * 512:(hf + 1) * 512], CK), k == 0, k == 3)
            c.copy(V(qn_t[:, st, hf * 512:(hf + 1) * 512], [("qn", st, hf)]), o)
            o = c.PSs(GEN, 0, 64, 0, 512)
            for k in range(4):
                c.mm(o, V(wuq_t[:, k, h * 192 + 128:h * 192 + 192], CK), V(cqn_t[:, k, hf * 512:(hf + 1) * 512], CK), k == 0, k == 3)
            qb = V(c.qrb.ap[0:64, :], c.qrb.kr(0, 512))
            t1 = V(c.tmp.ap[0:64, 0, :], c.tmp.kr(0, 512))
            t2 = V(c.tmp.ap[0:64, 1, :], c.tmp.kr(512, 512))
            c.copy(qb, o)
            c.tt(t1, o, V(cs_t[:, hf * 512:(hf + 1) * 512], CK), ALU.mult)
            o2 = c.PSs(GEN, 0, 64, 0, 512)
            c.mm(o2, c.rmat, qb, True, True)
            c.tt(t2, o2, V(sn_t[:, hf * 512:(hf + 1) * 512], CK), ALU.mult)
            c.tt(V(qr_t[:, st, hf * 512:(hf + 1) * 512], [("qr", st, hf)]), t1, t2, ALU.add)

    tiles = [(kc, hf) for kc in range(64) for hf in range(2)]

    def s_tile(h, i):
        kc, hf = tiles[i]
        st = h % 2
        b = (h * 128 + i) % 3
        o = c.PS(b)
        c.mm(o, V(KhT[st].ap[:, kc * 128:(kc + 1) * 128], KhT[st].kr(kc * 128, 128)), V(qn_t[:, st, hf * 512:(hf + 1) * 512], [("qn", st, hf)]), True, False)
        c.mm(o, V(kra_t[:, kc * 128:(kc + 1) * 128], [("kra", kc // 16)]), V(qr_t[:, st, hf * 512:(hf + 1) * 512], [("qr", st, hf)]), False, True)

    def pv_tile(h, i):
        kc, hf = tiles[i]
        st = h % 2
        b = (h * 128 + i) % 3
        p = c.Pt.v(b)
        c.act(p, c.PS(b), AF.Exp, scale=MLA_SCALE)
        c.mm(c.PS(3 + hf), V(Vh[st].ap[:, kc, :], Vh[st].kr(kc * 128, 128)), p, kc == 0, kc == 63)
        c.mm(c.PS(5 + hf), ones, p, kc == 0, kc == 63)

    load_kv(0)
    qprep(0)
    for h in range(8):
        if h + 1 < 8:
            load_kv(h + 1)
            qprep(h + 1)
        for i in range(128):
            if i == 0:
                s_tile(h, 0)
                s_tile(h, 1)
            if i + 2 < 128:
                s_tile(h, i + 2)
            pv_tile(h, i)
        for hf in range(2):
            out_y(1024 + h * 128, hf, 3 + hf, 5 + hf)
    return c.finish(), c.outs


def build_k3(final=False, chain=False):
    c = Ctx(nslot=4)
    xT_d = c.din("xT", [D, T], F32)
    yT_d = c.din("yT", [D, T], BF16)
    memT_d = c.din("memT", [D, 256], F32)
    gains_d = c.din("gains", [128, NG_L + 16], F32)
    cbf_d = c.din("cbf", [128, 320], BF16)
    wd_ = {k: c.din(k, [D, D], F32) for k in ("w_out", "w_xq", "w_xk", "w_xv", "w_xo")}
    wg_d = c.din("w_gate", [D, DFF], F32)
    wu_d = c.din("w_up", [D, DFF], F32)
    wdn_d = c.din("w_down", [DFF, D], F32)
    x_o = c.dout("xo", [D, T], F32)
    X_t = c.sb("X", [128, KC, T], F32)
    A_t = c.sb("A", [128, 16384], BF16)
    B_t = c.sb("B", [128, 16384], BF16)
    C_t = c.sb("C", [128, 12288], BF16)
    gains_t = c.sb("gains_sb", [128, NG_L + 16], F32)
    hT = Tile(A_t[:, :], "A", 0, [KC, T], BF16)
    yT = Tile(B_t[:, :], "B", 0, [KC, T], BF16)
    memn = Tile(C_t[:, :], "C", 16384, [KC, 256], BF16)
    aT = Tile(B_t[:, :], "B", 0, [4, T], BF16)
    mem32 = Tile(C_t[:, :], "C", 0, [KC, 256], F32)
    kmT = Tile(C_t[:, :], "C", 0, [KC, 256], BF16)
    vm = Tile(C_t[:, :], "C", 8192, [2, D], BF16)
    qm = Tile(C_t[:, :], "C", 16384, [4, T], BF16)
    xv = _xv(X_t)
    GK = [("gains",)]
    c.dma("sp", gains_t[:, :], gains_d.ap()[:, :], [], GK, grp="cin")
    c.load_consts(cbf_d)
    ones = c.ones

    def proj_resid(W_ap, src):
        for cb in range(8):
            wt = c.wtile(W_ap, cb * 256, 256)
            for oc in range(2):
                ko = cb * 2 + oc
                for hf in range(2):
                    b = c.nxt("ps", 4)
                    for k in range(KC):
                        c.mm(c.PS(b), V(wt.ap[:, k, oc * 128:(oc + 1) * 128], wt.keys), src.v(k, (hf * 512, hf * 512 + 512)), k == 0, k == KC - 1)
                    c.tt(xv(ko, hf), xv(ko, hf), c.PS(b), ALU.add)

    for k in range(KC):
        v = mem32.v(k)
        c.dma("sp", v.ap, memT_d.ap()[k * 128:(k + 1) * 128, :], [], v.keys, grp="memin")
    r = c.rms_stats(lambda k, hf_: mem32.v(k), KC, D, 0, width=256)
    for k in range(KC):
        c.stt(memn.v(k), mem32.v(k), gains_t[:, G_MEMN + k:G_MEMN + k + 1], r, ALU.mult, ALU.mult, extra=GK)
    for cb in range(8):
        wt = c.wtile(wd_["w_xk"].ap(), cb * 256, 256)
        for oc in range(2):
            o = c.PSs(c.nxt("ps", 4), 0, 128, 0, 256)
            for k in range(KC):
                c.mm(o, V(wt.ap[:, k, oc * 128:(oc + 1) * 128], wt.keys), memn.v(k), k == 0, k == KC - 1)
            c.copy(kmT.v(cb * 2 + oc), o)
    for cb in range(8):
        wt = c.wtile(wd_["w_xv"].ap(), cb * 256, 256)
        for mc in range(2):
            o = c.PSs(c.nxt("ps", 4), 0, 128, 0, 256)
            for k in range(KC):
                c.mm(o, V(memn.ap[:, k, mc * 128:(mc + 1) * 128], memn.kr(k * 256 + mc * 128, 128)), V(wt.ap[:, k, :], wt.keys), k == 0, k == KC - 1)
            c.copy(vm.v(mc, (cb * 256, cb * 256 + 256)), o)
    for k in range(KC):
        v = yT.v(k)
        c.dma("act", v.ap, yT_d.ap()[k * 128:(k + 1) * 128, :], [], v.keys, grp="yin")
    _load_x(c, X_t, xT_d)
    for hf in range(2):
        for g0, gofs in ((0, G_NAO), (8, G_MLAO)):
            r = c.rms_stats(lambda k, hf_, g0=g0: yT.v(g0 + k, (hf_ * 512, hf_ * 512 + 512)), 8, 1024, hf)
            for k in range(8):
                vv = yT.v(g0 + k, (hf * 512, hf * 512 + 512))
                c.stt(vv, vv, gains_t[:, gofs + k:gofs + k + 1], r, ALU.mult, ALU.mult, extra=GK)
    proj_resid(wd_["w_out"].ap(), yT)
    _rmsnorm_x(c, xv, gains_t, G_MEM, hT)
    for hd in range(4):
        for cbq in range(2):
            wt = c.wtile(wd_["w_xq"].ap(), hd * 512 + cbq * 256, 256)
            for oc in range(2):
                dc = cbq * 2 + oc
                for hf in range(2):
                    b = c.nxt("ps", 4)
                    for k in range(KC):
                        c.mm(c.PS(b), V(wt.ap[:, k, oc * 128:(oc + 1) * 128], wt.keys), hT.v(k, (hf * 512, hf * 512 + 512)), k == 0, k == KC - 1)
                    c.copy(qm.v(dc, (hf * 512, hf * 512 + 512)), c.PS(b))
        for hf in range(2):
            pts = []
            for mc in range(2):
                b = 4 + mc
                for dc in range(4):
                    c.mm(c.PS(b), V(kmT.ap[:, hd * 4 + dc, mc * 128:(mc + 1) * 128], kmT.kr((hd * 4 + dc) * 256 + mc * 128, 128)),
                         qm.v(dc, (hf * 512, hf * 512 + 512)), dc == 0, dc == 3)
                p = c.Pt.v(c.nxt("pt", 3))
                c.act(p, c.PS(b), AF.Exp, scale=X_SCALE)
                pts.append(p)
            for mc in range(2):
                c.mm(c.PS(6), ones, pts[mc], mc == 0, mc == 1)
            r = c.rr.v(c.nxt("rr", 2))
            c.recip(r, c.PS(6))
            for dc in range(4):
                b = c.nxt("ps", 4)
                for mc in range(2):
                    f0 = (hd * 4 + dc) * 128
                    c.mm(c.PS(b), V(vm.ap[:, mc, f0:f0 + 128], vm.kr(mc * D + f0, 128)), pts[mc], mc == 0, mc == 1)
                c.tt(yT.v(hd * 4 + dc, (hf * 512, hf * 512 + 512)), c.PS(b), r, ALU.mult)
    proj_resid(wd_["w_xo"].ap(), yT)
    _rmsnorm_x(c, xv, gains_t, G_FFN, hT)
    for fb in range(11):
        for cbf_ in range(2):
            wg = c.wtile(wg_d.ap(), fb * 512 + cbf_ * 256, 256)
            wu = c.wtile(wu_d.ap(), fb * 512 + cbf_ * 256, 256)
            for oc in range(2):
                fc = cbf_ * 2 + oc
                for hf in range(2):
                    bg = 2 * c.nxt("gu", 2)
                    for k in range(KC):
                        c.mm(c.PS(bg), V(wg.ap[:, k, oc * 128:(oc + 1) * 128], wg.keys), hT.v(k, (hf * 512, hf * 512 + 512)), k == 0, k == KC - 1)
                    for k in range(KC):
                        c.mm(c.PS(bg + 1), V(wu.ap[:, k, oc * 128:(oc + 1) * 128], wu.keys), hT.v(k, (hf * 512, hf * 512 + 512)), k == 0, k == KC - 1)
                    sg = c.tmp.v(c.nxt("tmp", 2))
                    c.act(sg, c.PS(bg), AF.Silu)
                    c.tt(aT.v(fc, (hf * 512, hf * 512 + 512)), sg, c.PS(bg + 1), ALU.mult)
        for dh in range(2):
            wt = c.wtile(wdn_d.ap()[fb * 512:(fb + 1) * 512, :], dh * 1024, 1024, kc=4)
            for dk in range(8):
                ko = dh * 8 + dk
                for hf in range(2):
                    b = 4 + c.nxt("dn", 4)
                    for fc in range(4):
                        c.mm(c.PS(b), V(wt.ap[:, fc, dk * 128:(dk + 1) * 128], wt.keys), aT.v(fc, (hf * 512, hf * 512 + 512)), fc == 0, fc == 3)
                    c.tt(xv(ko, hf), xv(ko, hf), c.PS(b), ALU.add)
                if fb == 10 and not final:
                    c.dma("sp", x_o.ap()[ko * 128:(ko + 1) * 128, :], X_t[:, ko, :], [("X", ko, 0), ("X", ko, 1)], [("out", ko)], grp="xo")
    if final:
        for hf in range(2):
            r = c.rms_stats(xv, KC, D, hf)
            for k in range(KC):
                t = c.tmp.v(c.nxt("tmp", 2))
                c.stt(t, xv(k, hf), gains_t[:, NG_L + k:NG_L + k + 1], r, ALU.mult, ALU.mult, extra=GK)
                c.dma("sp", x_o.ap()[k * 128:(k + 1) * 128, hf * 512:(hf + 1) * 512], t.ap, t.keys, [("out", k, hf)], grp="xo")
    if chain:
        return build_k1(c=c, X_t=X_t, A_t=A_t, B_t=B_t, C_t=C_t)
    return c.finish(), c.outs


def _bf(a):
    return np.ascontiguousarray(a).astype(ml_dtypes.bfloat16)


_PROGS = {}


def _prog(name):
    if name not in _PROGS:
        _PROGS[name] = {"k1": build_k1, "k2": build_k2, "k3": lambda: build_k3(False), "k3f": lambda: build_k3(True),
                         "k31": lambda: build_k3(False, chain=True)}[name]()
    return _PROGS[name]


def _consts():
    cbf = np.zeros((128, 320), np.float32)
    cbf[:, 0:128] = 1.0
    cbf[:, 128:256] = np.eye(128, dtype=np.float32)
    for m in range(32):
        cbf[m + 32, 256 + m] = -1.0
        cbf[m, 256 + 32 + m] = 1.0
    ind2 = np.zeros((2, 128), np.float32)
    ind2[0, 0:64] = 1.0
    ind2[1, 64:128] = 1.0
    inv = (1.0 / (10000.0 ** (np.arange(0, 64, 2, dtype=np.float32) / np.float32(64)))).astype(np.float32)
    cs, sn, rms = [], [], []
    for c in range(NC_):
        pos = (c * T + np.arange(T)).astype(np.float32)
        ang = (pos[:, None] * inv[None, :]).astype(np.float32)
        cs.append(np.concatenate([np.cos(ang).T, np.cos(ang).T], 0).astype(np.float32))
        sn.append(np.concatenate([np.sin(ang).T, np.sin(ang).T], 0).astype(np.float32))
        R0 = 16 * c
        rm = np.zeros((2, 56, 2, 64), np.float32)
        for b in range(8):
            for jj in range(7):
                for a in range(2):
                    for e in range(2):
                        kr_ = R0 - 4 + 2 * (b + jj - 1) + a
                        qr_ = R0 + 2 * b + e
                        rs_ = min(max(qr_ - 4, 0), 120)
                        ok = (0 <= kr_ <= 127) and (rs_ <= kr_ < rs_ + 8)
                        rm[a, b * 7 + jj, e, :] = 0.0 if ok else NEG
        rms.append(_bf(rm.reshape(2, 7168)))
    return _bf(cbf), _bf(ind2), cs, sn, rms


def _tt_tiles(rpb_l):
    kc = np.arange(64)
    col_start = np.clip(kc - 8, 0, 48)
    colvalid = (kc[:, None] >= col_start[None, :]) & (kc[:, None] < col_start[None, :] + 16)
    dcc = np.clip(kc[:, None] - kc[None, :] + 15, 0, 30)
    tt = np.full((8, 128, 7 * 128), NEG, np.float32)
    for jj in range(7):
        for a in range(2):
            for e in range(2):
                dr = 2 * (jj - 1) + a - e - 4 + 7
                if 0 <= dr <= 14:
                    blk = rpb_l[:, dr][:, dcc]
                    blk = np.where(colvalid[None], blk, np.float32(NEG))
                    tt[:, a * 64:(a + 1) * 64, jj * 128 + e * 64: jj * 128 + (e + 1) * 64] = blk
    return tt


def _gains_layer(inp, l, with_final=False):
    f = lambda k: np.asarray(inp[k], np.float32)
    G = np.zeros((128, NG_L + (16 if with_final else 0)), np.float32)

    def put(base, vec):
        nk = vec.shape[0] // 128
        G[:, base:base + nk] = vec.reshape(nk, 128).T
    put(G_MIX, f("ln_mix")[l]); put(G_MEM, f("ln_mem")[l]); put(G_MEMN, f("mem_norm")[l]); put(G_FFN, f("ln_ffn")[l])
    put(G_QN, f("q_norm")[l]); put(G_KVN, f("kv_norm")[l]); put(G_NAO, f("na_out_norm")[l]); put(G_MLAO, f("mla_out_norm")[l])
    if with_final:
        put(NG_L, f("final_norm"))
    return G


def _launch(name, maps):
    nc, outs = _prog(name)
    res = run_bass_kernel_spmd(nc, maps, core_ids=list(range(NC_)))
    return [{k: np.asarray(r[k]) for k in outs} for r in res.results]


def _k1_maps(inp, l, cst, chained):
    cbf, ind2, cs, sn, rms = cst
    f = lambda k: np.ascontiguousarray(np.asarray(inp[k], np.float32)[l])
    w_in, w_ukv, G1 = f("w_in"), f("w_ukv"), _gains_layer(inp, l)
    gname = "gains1" if chained else "gains"
    return [{"w_in": w_in, "w_ukv": w_ukv, gname: G1, "cs": cs[c], "sn": sn[c]} for c in range(NC_)]


def _k2_maps(inp, l, r1, cst):
    cbf, ind2, cs, sn, rms = cst
    f = lambda k: np.ascontiguousarray(np.asarray(inp[k], np.float32)[l])
    kr_all = np.ascontiguousarray(np.concatenate([r["kr"] for r in r1], axis=1))
    k_all = np.ascontiguousarray(np.concatenate([r["kmla"] for r in r1], axis=1).reshape(8, 128, S))
    v_cat = np.concatenate([r["vmla"] for r in r1], axis=0)
    v_all = np.ascontiguousarray(v_cat.reshape(64, 128, 8, 128).transpose(2, 1, 0, 3).reshape(8, 128, S))
    tt = _tt_tiles(np.asarray(inp["na_rpb"], np.float32)[l])
    w_uq = f("w_uq")
    maps2 = []
    for c in range(NC_):
        p, n = max(c - 1, 0), min(c + 1, NC_ - 1)
        knah = np.concatenate([r1[p]["kna"][:, 768:], r1[c]["kna"], r1[n]["kna"][:, :256]], axis=1)
        vnah = np.concatenate([r1[p]["vna"][768:], r1[c]["vna"], r1[n]["vna"][:256]], axis=0)
        maps2.append({"qna": r1[c]["qna"], "knah": np.ascontiguousarray(knah), "vnah": np.ascontiguousarray(vnah),
                      "cqn": r1[c]["cqn"], "k_all": k_all, "v_all": v_all, "kr_all": kr_all, "w_uq": w_uq, "tt": tt,
                      "rms": rms[c], "ind2": ind2, "cs": cs[c], "sn": sn[c], "cbf": cbf})
    return maps2


def _k3_maps(inp, l, xTs, r2, cst):
    cbf = cst[0]
    f = lambda k: np.ascontiguousarray(np.asarray(inp[k], np.float32)[l])
    G3 = _gains_layer(inp, l, with_final=True)
    memT = np.ascontiguousarray(np.asarray(inp["mem"], np.float32)[0].T)
    wk = {k: f(k) for k in ("w_out", "w_xq", "w_xk", "w_xv", "w_xo", "w_gate", "w_up", "w_down")}
    return [dict(wk, xT=xTs[c], yT=r2[c]["yT"], memT=memT, gains=G3, cbf=cbf) for c in range(NC_)]


def kernel(**inputs):
    x = np.asarray(inputs["x"], np.float32)[0]
    cst = _consts()
    cbf = cst[0]
    xTs = [np.ascontiguousarray(x[c * T:(c + 1) * T].T) for c in range(NC_)]
    m1 = _k1_maps(inputs, 0, cst, chained=False)
    r1 = _launch("k1", [dict(m1[c], xT=xTs[c], cbf=cbf) for c in range(NC_)])
    for l in range(L):
        r2 = _launch("k2", _k2_maps(inputs, l, r1, cst))
        m3 = _k3_maps(inputs, l, xTs, r2, cst)
        if l + 1 < L:
            m1 = _k1_maps(inputs, l + 1, cst, chained=True)
            r1 = _launch("k31", [dict(m3[c], **m1[c]) for c in range(NC_)])
            xTs = [r["xo"] for r in r1]
        else:
            xTs = [r["xo"] for r in _launch("k3f", m3)]
    out = np.concatenate([a.T for a in xTs], axis=0)
    return np.ascontiguousarray(out.reshape(1, S, D).astype(np.float32))
```
